# Optimizing a Trainium2 kernel written in Bass

```python
import jax
import jax.numpy as jnp
from jax import lax
import numpy as np


D_MODEL = 1024
BATCH = 16
SEQ = 2048
DEPTH = 2

GRID_W = 64
CTX_LEN = 256
HEAD_DIM = 64
N_HEADS_A = 8
N_KV_A = 2
N_HEADS_B = 8
N_KV_B = 2
MIX_WIDTH = (N_HEADS_A + N_HEADS_B) * HEAD_DIM
Q_BLOCK = 128
WINDOW = 128
D_FF = 4 * D_MODEL
ROPE_THETA = 10000.0
EPS = 1e-6
NEG_BIG = -1e30
N_MOD = 6
COL_SIZES = (N_HEADS_A * HEAD_DIM, N_KV_A * HEAD_DIM, N_KV_A * HEAD_DIM,
             N_HEADS_B * HEAD_DIM, N_KV_B * HEAD_DIM, N_KV_B * HEAD_DIM)
IN_COLS = sum(COL_SIZES)
SPLITS = tuple(int(s) for s in np.cumsum(COL_SIZES)[:-1])

kernel_name = 'hybrid_dit_gqa_window_sink_block'


def rms_norm(x, g):
    xf = x.astype(jnp.float32)
    y = xf * lax.rsqrt(jnp.mean(xf * xf, axis=-1, keepdims=True) + EPS)
    return (y * g.astype(jnp.float32)).astype(x.dtype)


def modulate(x, shift, scale):
    return x * (1 + scale) + shift


def axial_rope_tables(n_tok):
    rows = n_tok // GRID_W
    row_ids = jnp.repeat(jnp.arange(rows, dtype=jnp.int32), GRID_W).astype(jnp.float32)
    col_ids = jnp.tile(jnp.arange(GRID_W, dtype=jnp.int32), rows).astype(jnp.float32)
    axis_dim = HEAD_DIM // 2
    inv = ROPE_THETA ** (-jnp.arange(0, axis_dim, 2, dtype=jnp.float32) / axis_dim)
    ang_r = row_ids[:, None] * inv[None, :]
    ang_c = col_ids[:, None] * inv[None, :]
    return (jnp.cos(ang_r), jnp.sin(ang_r), jnp.cos(ang_c), jnp.sin(ang_c))


def _rotate(a, cos, sin):
    a1, a2 = jnp.split(a, 2, axis=-1)
    return jnp.concatenate([a1 * cos - a2 * sin, a2 * cos + a1 * sin], axis=-1)


def apply_axial_rope(t, rope):
    cos_r, sin_r, cos_c, sin_c = rope
    tf = t.astype(jnp.float32)
    tr, tc = jnp.split(tf, 2, axis=-1)
    return jnp.concatenate([_rotate(tr, cos_r, sin_r), _rotate(tc, cos_c, sin_c)], axis=-1).astype(t.dtype)


def to_gqa(t, n_heads, n_kv):
    b, n, _ = t.shape
    return t.reshape(b, n, n_kv, n_heads // n_kv, HEAD_DIM).transpose(0, 2, 3, 1, 4)


def to_kv(t, n_kv):
    b, n, _ = t.shape
    return t.reshape(b, n, n_kv, HEAD_DIM).transpose(0, 2, 1, 3)


def merge_heads(o):
    b, hk, g, n, d = o.shape
    return o.transpose(0, 3, 1, 2, 4).reshape(b, n, hk * g * d)


def multi_source_attention(q, ks, vs, biases, sink):
    scale = HEAD_DIM ** -0.5
    logits = [jnp.einsum('bhgqd,bhkd->bhgqk', q, k, preferred_element_type=jnp.float32) * scale + bias
              for k, bias in zip(ks, biases)]
    sizes = [l.shape[-1] for l in logits]
    if sink is not None:
        s = sink.astype(jnp.float32)[None, :, :, None, None]
        logits.append(jnp.broadcast_to(s, logits[0].shape[:-1] + (1,)))
    p = jax.nn.softmax(jnp.concatenate(logits, axis=-1), axis=-1)
    out = None
    off = 0
    for v, n in zip(vs, sizes):
        term = jnp.einsum('bhgqk,bhkd->bhgqd', p[..., off:off + n].astype(v.dtype), v)
        out = term if out is None else out + term
        off += n
    return out


def dense_latent_attention(q, k_lat, v_lat, k_ctx, v_ctx):
    b, hk, g, s, d = q.shape
    nblk = s // Q_BLOCK
    qb = jnp.moveaxis(q.reshape(b, hk, g, nblk, Q_BLOCK, d), 3, 0)

    def one_block(qi):
        return multi_source_attention(qi, (k_ctx, k_lat), (v_ctx, v_lat), (0.0, 0.0), None)

    o = lax.map(one_block, qb)
    return jnp.moveaxis(o, 0, 3).reshape(b, hk, g, s, d)


def banded_blocks(t):
    b, hk, s, d = t.shape
    nblk = s // Q_BLOCK
    tp = jnp.pad(t, ((0, 0), (0, 0), (Q_BLOCK, Q_BLOCK), (0, 0))).reshape(b, hk, nblk + 2, Q_BLOCK, d)
    band = jnp.concatenate([tp[:, :, :-2], tp[:, :, 1:-1], tp[:, :, 2:]], axis=3)
    return jnp.moveaxis(band, 2, 0)


def window_bias(s):
    nblk = s // Q_BLOCK
    i = jnp.arange(nblk, dtype=jnp.int32)[:, None, None]
    r = jnp.arange(Q_BLOCK, dtype=jnp.int32)[None, :, None]
    j = jnp.arange(3 * Q_BLOCK, dtype=jnp.int32)[None, None, :]
    qpos = i * Q_BLOCK + r
    kpos = (i - 1) * Q_BLOCK + j
    valid = (jnp.abs(kpos - qpos) <= WINDOW) & (kpos >= 0) & (kpos < s)
    return jnp.where(valid, 0.0, NEG_BIG).astype(jnp.float32)


def windowed_latent_attention(q, k_lat, v_lat, k_ctx, v_ctx, sink):
    b, hk, g, s, d = q.shape
    nblk = s // Q_BLOCK
    qb = jnp.moveaxis(q.reshape(b, hk, g, nblk, Q_BLOCK, d), 3, 0)
    kb = banded_blocks(k_lat)
    vb = banded_blocks(v_lat)
    bias = window_bias(s)

    def one_block(args):
        qi, ki, vi, bi = args
        return multi_source_attention(qi, (ki, k_ctx), (vi, v_ctx), (bi[None, None, None], 0.0), sink)

    o = lax.map(one_block, (qb, kb, vb, bias))
    return jnp.moveaxis(o, 0, 3).reshape(b, hk, g, s, d)


def sq_relu_mlp(u, w_up, w_down):
    return jnp.square(jax.nn.relu(u @ w_up)) @ w_down


def setup_inputs(seed: int = 0) -> dict:
    key = jax.random.key(seed)
    ks = jax.random.split(key, 18)
    f32 = jnp.float32
    nrm = lambda k, shape, s: jax.random.normal(k, shape, f32) * s
    return {
        'x': nrm(ks[0], (BATCH, SEQ, D_MODEL), 1.0),
        'c': nrm(ks[1], (BATCH, D_MODEL), 1.0),
        'ctx': nrm(ks[2], (BATCH, CTX_LEN, D_MODEL), 1.0),
        'c_ctx': nrm(ks[3], (D_MODEL,), 1.0),
        'w_ada': nrm(ks[4], (DEPTH, D_MODEL, N_MOD * D_MODEL), 0.02),
        'b_ada': nrm(ks[5], (DEPTH, N_MOD * D_MODEL), 0.01),
        'g_pre_mix': 1.0 + nrm(ks[6], (DEPTH, D_MODEL), 0.05),
        'g_post_mix': 1.0 + nrm(ks[7], (DEPTH, D_MODEL), 0.05),
        'g_pre_mlp': 1.0 + nrm(ks[8], (DEPTH, D_MODEL), 0.05),
        'g_post_mlp': 1.0 + nrm(ks[9], (DEPTH, D_MODEL), 0.05),
        'w_in': nrm(ks[10], (DEPTH, D_MODEL, IN_COLS), D_MODEL ** -0.5),
        'q_norm': 1.0 + nrm(ks[11], (DEPTH, HEAD_DIM), 0.05),
        'k_norm': 1.0 + nrm(ks[12], (DEPTH, HEAD_DIM), 0.05),
        'sink': nrm(ks[13], (DEPTH, N_HEADS_B), 0.5),
        'w_out': nrm(ks[14], (DEPTH, MIX_WIDTH, D_MODEL), MIX_WIDTH ** -0.5),
        'w_up': nrm(ks[15], (DEPTH, D_MODEL, D_FF), D_MODEL ** -0.5),
        'w_down': nrm(ks[16], (DEPTH, D_FF, D_MODEL), D_FF ** -0.5),
    }


def reference(x, c, ctx, c_ctx, w_ada, b_ada, g_pre_mix, g_post_mix, g_pre_mlp, g_post_mlp,
              w_in, q_norm, k_norm, sink, w_out, w_up, w_down):
    b, s, d = x.shape
    rope = axial_rope_tables(s)
    silu_c = jax.nn.silu(c)
    silu_cc = jax.nn.silu(c_ctx)
    h, hc = x, ctx
    for l in range(DEPTH):
        last = l == DEPTH - 1
        mod = (silu_c @ w_ada[l] + b_ada[l]).reshape(b, N_MOD, 1, d)
        mod_c = (silu_cc @ w_ada[l] + b_ada[l]).reshape(N_MOD, 1, 1, d)
        sh_a, sc_a, g_a, sh_m, sc_m, g_m = [mod[:, i] for i in range(N_MOD)]
        csh_a, csc_a, cg_a, csh_m, csc_m, cg_m = [mod_c[i] for i in range(N_MOD)]

        u = modulate(rms_norm(h, g_pre_mix[l]), sh_a, sc_a)
        uc = modulate(rms_norm(hc, g_pre_mix[l]), csh_a, csc_a)
        qa, ka, va, qb, kb, vb = jnp.split(u @ w_in[l], SPLITS, axis=-1)
        qac, kac, vac, qbc, kbc, vbc = jnp.split(uc @ w_in[l], SPLITS, axis=-1)

        qa = apply_axial_rope(rms_norm(to_gqa(qa, N_HEADS_A, N_KV_A), q_norm[l]), rope)
        ka = apply_axial_rope(rms_norm(to_kv(ka, N_KV_A), k_norm[l]), rope)
        va = to_kv(va, N_KV_A)
        kac = rms_norm(to_kv(kac, N_KV_A), k_norm[l])
        vac = to_kv(vac, N_KV_A)
        qb = apply_axial_rope(to_gqa(qb, N_HEADS_B, N_KV_B), rope)
        kb = apply_axial_rope(to_kv(kb, N_KV_B), rope)
        vb = to_kv(vb, N_KV_B)
        kbc = to_kv(kbc, N_KV_B)
        vbc = to_kv(vbc, N_KV_B)
        sink_l = sink[l].reshape(N_KV_B, N_HEADS_B // N_KV_B)

        oa = dense_latent_attention(qa, ka, va, kac, vac)
        ob = windowed_latent_attention(qb, kb, vb, kbc, vbc, sink_l)
        mix = jnp.concatenate([merge_heads(oa), merge_heads(ob)], axis=-1) @ w_out[l]
        h = h + g_a * rms_norm(mix, g_post_mix[l])

        if not last:
            qac = rms_norm(to_gqa(qac, N_HEADS_A, N_KV_A), q_norm[l])
            oac = multi_source_attention(qac, (kac,), (vac,), (0.0,), None)
            obc = multi_source_attention(to_gqa(qbc, N_HEADS_B, N_KV_B), (kbc,), (vbc,), (0.0,), sink_l)
            mixc = jnp.concatenate([merge_heads(oac), merge_heads(obc)], axis=-1) @ w_out[l]
            hc = hc + cg_a * rms_norm(mixc, g_post_mix[l])

        y = sq_relu_mlp(modulate(rms_norm(h, g_pre_mlp[l]), sh_m, sc_m), w_up[l], w_down[l])
        h = h + g_m * rms_norm(y, g_post_mlp[l])
        if not last:
            yc = sq_relu_mlp(modulate(rms_norm(hc, g_pre_mlp[l]), csh_m, csc_m), w_up[l], w_down[l])
            hc = hc + cg_m * rms_norm(yc, g_post_mlp[l])
    return h
```

```python
import contextlib
import numpy as np
import concourse.bass as bass
import concourse.mybir as mybir
from concourse.bass_utils import run_bass_kernel_spmd

F32 = mybir.dt.float32
BF16 = mybir.dt.bfloat16
U8 = mybir.dt.uint8
ALU = mybir.AluOpType
AF = mybir.ActivationFunctionType
AX = mybir.AxisListType

D = 1024
SEQ = 2048
CTX = 256
NT = 16
NCT = 2
NTT = NT + NCT
DFF = 4096
EPS = 1e-6
NEG = -30000.0
NCORES = 8
BPC = 2

DEV = {}

ENGS = ["pe", "act", "dve", "pool", "sp"]


class Sched:
    def __init__(self, nc):
        self.nc = nc
        self.ops = {e: [] for e in ENGS}
        self.last_w = {}
        self.readers = {}
        self.dma_cnt = {}
        self.dma_keys = []
        self.waited = {e: {} for e in ENGS}

    def _add_dep(self, eng, deps, tok, same_ok):
        if tok is None:
            return
        if tok[0] == 'e' and tok[1] == eng and (not same_ok or eng == 'pe'):
            return
        src = (tok[0], tok[1])
        val = tok[2]
        if tok[0] == 'd':
            val = self.dma_cnt[tok[1]]
        if val <= self.waited[eng].get(src, -1):
            return
        if val > deps.get(src, -1):
            deps[src] = val

    def op(self, eng, fn, reads=(), writes=(), dma_key=None):
        deps = {}
        for r in reads:
            self._add_dep(eng, deps, self.last_w.get(r), True)
        for r in writes:
            self._add_dep(eng, deps, self.last_w.get(r), False)
            for t in self.readers.get(r, ()):
                self._add_dep(eng, deps, t, False)
        for src, v in deps.items():
            self.waited[eng][src] = v
        idx = len(self.ops[eng])
        if dma_key is not None:
            if dma_key not in self.dma_cnt:
                self.dma_cnt[dma_key] = 0
                self.dma_keys.append(dma_key)
            self.dma_cnt[dma_key] += 1
            tok = ('d', dma_key, self.dma_cnt[dma_key])
        else:
            tok = ('e', eng, idx)
        self.ops[eng].append(dict(fn=fn, deps=deps, dma_key=dma_key, signal=False))
        if fn is None:
            return None
        for r in reads:
            self.readers.setdefault(r, []).append(tok)
        for r in writes:
            self.last_w[r] = tok
            self.readers[r] = []
        return tok

    def barrier(self, skip=lambda k: False):
        last = {}
        for e in ENGS:
            for i in range(len(self.ops[e]) - 1, -1, -1):
                o = self.ops[e][i]
                if o['fn'] is not None and o['dma_key'] is None:
                    last[e] = i
                    break
        for e in ENGS:
            deps = {}
            for src, i in last.items():
                if src != e and i > self.waited[e].get(('e', src), -1):
                    deps[('e', src)] = i
            for k, c in self.dma_cnt.items():
                if skip(k):
                    continue
                if c > self.waited[e].get(('d', k), -1):
                    deps[('d', k)] = c
            for src, v in deps.items():
                self.waited[e][src] = v
            self.ops[e].append(dict(fn=None, deps=deps, dma_key=None, signal=False))

    def wait_dma_all(self, eng, key):
        self.ops[eng].append(dict(fn=None, deps={('d', key): self.dma_cnt[key]}, dma_key=None, signal=False))

    def emit(self):
        nc = self.nc
        for e in ENGS:
            for o in self.ops[e]:
                for (kind, src), v in o['deps'].items():
                    if kind == 'e':
                        self.ops[src][v]['signal'] = True
        rank = {e: {} for e in ENGS}
        for e in ENGS:
            c = 0
            for i, o in enumerate(self.ops[e]):
                if o['signal']:
                    assert o['dma_key'] is None and o['fn'] is not None
                    c += 1
                    rank[e][i] = c
        with contextlib.ExitStack() as st:
            esem = {e: st.enter_context(nc.semaphore("s_" + e)) for e in ENGS}
            dsem = {k: st.enter_context(nc.semaphore("d_%d" % i)) for i, k in enumerate(self.dma_keys)}
            block = st.enter_context(nc.Block())

            def body(e):
                def run(engine):
                    for i, o in enumerate(self.ops[e]):
                        for (kind, src), v in o['deps'].items():
                            if kind == 'e':
                                engine.wait_ge(esem[src], rank[src][v])
                            else:
                                engine.wait_ge(dsem[src], 16 * v)
                        if o['fn'] is None:
                            continue
                        ins = o['fn'](engine)
                        if o['dma_key'] is not None:
                            ins.then_inc(dsem[o['dma_key']], 16)
                        elif o['signal']:
                            ins.then_inc(esem[e], 1)
                return run

            block.tensor(body("pe"))
            block.scalar(body("act"))
            block.vector(body("dve"))
            block.gpsimd(body("pool"))
            block.sync(body("sp"))
        self.stats = {e: (len(self.ops[e]), len(rank[e])) for e in ENGS}


def _dtsize(dt):
    return {F32: 4, BF16: 2, U8: 1}[dt]


class Arena:
    def __init__(self, t, cap):
        self.t = t
        self.cap = cap
        self.off = 0

    def mark(self):
        return self.off

    def reset(self, m):
        self.off = m

    def alloc(self, free_shape, dt):
        n = int(np.prod(free_shape)) * _dtsize(dt)
        off = (self.off + 31) // 32 * 32
        assert off + n <= self.cap, ("SBUF arena overflow", off + n, self.cap)
        self.off = off + n
        ap = self.t[:, off:off + n].bitcast(dt)
        if len(free_shape) == 2:
            ap = ap.rearrange("p (a b) -> p a b", a=free_shape[0])
        elif len(free_shape) == 3:
            ap = ap.rearrange("p (a b c) -> p a b c", a=free_shape[0], b=free_shape[1])
        elif len(free_shape) == 4:
            ap = ap.rearrange("p (a b c d) -> p a b c d", a=free_shape[0], b=free_shape[1], c=free_shape[2])
        return ap


def build_program(L, is_last, emit_hc):
    nc = bass.Bass("TRN2", target_bir_lowering=False)

    def din(name, shape, dt=F32):
        return nc.dram_tensor(name, list(shape), dt, kind="ExternalInput").ap()

    x_in = din("x_in", [BPC, SEQ, D])
    ctx_in = din("ctx_in", [BPC, CTX, D])
    cT_d = din("cT", [128, 8, 4])
    w_ada_d = din("w_ada", [L, D, 6 * D])
    b_adaT_d = din("b_adaT", [128, L, 48])
    gT_d = din("gT", [128, L, 4, 8])
    qk_d = din("qk_norm", [L * 4 * 64])
    sink_d = din("sink", [L * 8])
    w_in_d = din("w_in", [L, D, 1536])
    w_out_d = din("w_out", [L, D, D])
    w_up_d = din("w_up", [L, D, DFF])
    w_down_d = din("w_down", [L, DFF, D])
    ident_d = din("ident", [128, 128])
    rope_d = din("rope_cs", [128, 17, 2, 64])
    masks_d = din("masks", [128, 2, 512])
    h_out = nc.dram_tensor("h_out", [BPC, SEQ, D], F32, kind="ExternalOutput").ap()
    hc_out = None
    if emit_hc:
        hc_out = nc.dram_tensor("hc_out", [BPC, CTX, D], F32, kind="ExternalOutput").ap()

    def dscr(name, shape, dt):
        return nc.dram_tensor(name, list(shape), dt, kind="Internal").ap()

    w_in_s = dscr("w_in_s", [L, D, 1536], BF16)
    w_out_s = dscr("w_out_s", [L, D, D], BF16)
    w_up_s = dscr("w_up_s", [L, D, DFF], BF16)
    w_down_s = dscr("w_down_s", [L, DFF, D], BF16)
    tabs_s = dscr("tabs_s", [L, 17, 128, 6 * 64], F32)

    dbg_outs = []

    with contextlib.ExitStack() as st:
        CAP = 212000
        sbt = st.enter_context(nc.sbuf_tensor("sb_all", [128, CAP], U8))
        ps = st.enter_context(nc.psum_tensor("ps_all", [128, 8, 512], F32))
        A = Arena(sbt, CAP)
        S = Sched(nc)

        pool_hist = []
        POOL_MAX_INFLIGHT = 4

        def do(eng, method, *args, R=(), W=(), K=None, **kw):
            if eng == "pool" and K is not None and len(pool_hist) >= POOL_MAX_INFLIGHT:
                tk = pool_hist[-POOL_MAX_INFLIGHT]
                if tk[2] > S.waited["pool"].get(('d', tk[1]), -1):
                    S.ops["pool"].append(dict(fn=None, deps={('d', tk[1]): tk[2]}, dma_key=None, signal=False))
                    S.waited["pool"][('d', tk[1])] = tk[2]
            tok = S.op(eng, lambda e: getattr(e, method)(*args, **kw), reads=list(R), writes=list(W), dma_key=K)
            if eng == "pool" and K is not None:
                pool_hist.append(tok)

        def bank(i):
            return ps[:, i, :]

        def bank_bf(i):
            return ps[:, i, :].bitcast(BF16).rearrange("p (a b) -> p a b", a=8)

        def BK(i):
            return ("bk", i)

        h_sb = A.alloc([NT, D], F32)
        hc_sb = A.alloc([NCT, D], F32)
        ident_bf = A.alloc([128], BF16)
        ident_f = A.alloc([128], F32)
        ones_f = A.alloc([128], F32)
        mask_bf = A.alloc([2, 512], BF16)
        cT_sb = A.alloc([8, 4], F32)
        sT = A.alloc([8, 4], F32)
        modT = A.alloc([L, 48, 4], F32)
        cols = A.alloc([L, 4, 8, 4], F32)
        gT_sb = A.alloc([L, 4, 8], F32)
        b_adaT_sb = A.alloc([L, 48], F32)
        qk_b = A.alloc([L, 4, 64], F32)
        sink_b = A.alloc([L, 8], F32)
        sinkexp = A.alloc([L, 8], F32)
        Grow = A.alloc([4, D], F32)
        rstd1 = A.alloc([NTT], F32)
        rstd2 = A.alloc([NTT], F32)
        sm = A.alloc([64], F32)
        tabslot = A.alloc([3, 6, 64], F32)
        persist_mark = A.mark()

        def dbg(name, ap, shape, dt, reads):
            if not DEV.get("dump"):
                return
            o = nc.dram_tensor("dbg_" + name, list(shape), dt, kind="ExternalOutput").ap()
            do("sp", "dma_start", out=o, in_=ap, R=reads, K="dbg")
            dbg_outs.append("dbg_" + name)

        def conv(dst, src, rows, key):
            for r0 in range(0, rows, 128):
                do("pool", "dma_start", out=dst[r0:r0 + 128, :], in_=src[r0:r0 + 128, :], max_dma_last_dim=8192,
                   W=[(key, r0)], K=key)

        wa_bf = [A.alloc([8, 768], BF16) for _ in range(4)]
        sT_bf = A.alloc([8, 4], BF16)

        def load_wada(l, slab):
            if True:
                do("pool", "dma_start", out=wa_bf[slab % 4],
                   in_=w_ada_d[l][:, slab * 768:(slab + 1) * 768].rearrange("(c p) f -> p c f", p=128),
                   max_dma_last_dim=8192, W=[("wa", slab % 4)], K=("wa", slab % 4))

        def cv_res(name, l, rows):
            return [((name, l), r0) for r0 in range(0, rows, 128)]

        tab_sb = A.alloc([17, 6, 64], F32)
        rope_gen = A.alloc([17, 2, 64], F32)
        masks_f = A.alloc([2, 512], F32)
        tmp84 = A.alloc([8, 4], F32)

        def ld(dst, src, res):
            do("sp", "dma_start", out=dst, in_=src, W=[res], K="c")

        ld(ident_f, ident_d, "ident_f")
        ld(cT_sb, cT_d, "cT_sb")
        ld(b_adaT_sb, b_adaT_d, "b_adaT_sb")
        ld(gT_sb, gT_d, "gT_sb")
        ld(rope_gen, rope_d, "rope_gen")
        ld(masks_f, masks_d, "masks_f")
        ld(qk_b.rearrange("p a b c -> p (a b c)"), qk_d.partition_broadcast(128), "qk_b")
        ld(sink_b.rearrange("p a b -> p (a b)"), sink_d.partition_broadcast(128), "sink_b")

        do("dve", "tensor_copy", out=ident_bf, in_=ident_f, R=["ident_f"], W=["ident_bf"])
        do("dve", "tensor_copy", out=mask_bf, in_=masks_f, R=["masks_f"], W=["mask_bf"])
        do("dve", "memset", ones_f, 1.0, W=["ones_f"])
        do("act", "activation", out=sT, in_=cT_sb, func=AF.Silu, R=["cT_sb"], W=["sT"])
        do("dve", "tensor_copy", out=sT_bf, in_=sT, R=["sT"], W=["sT_bf"])
        do("act", "activation", out=sinkexp, in_=sink_b, func=AF.Exp, R=["sink_b"], W=["sinkexp"])

        modps = bank(0)[:, 0:192].rearrange("p (a b) -> p a b", b=4)
        def mod_layer(l):
            for slab in range(4):
                load_wada(l, slab)
            if l == 0:
                conv(w_in_s[0], w_in_d[0], D, ("cv_in", 0))
                conv(w_out_s[0], w_out_d[0], D, ("cv_out", 0))
            for slab in range(8):
                wb = wa_bf[slab % 4]
                for fl in range(6):
                    ch = slab * 6 + fl
                    for c in range(8):
                        do("pe", "matmul", modps[:, ch, :], lhsT=wb[:, c, fl * 128:(fl + 1) * 128], rhs=sT_bf[:, c, :],
                           start=(c == 0), stop=(c == 7), R=[("wa", slab % 4), "sT_bf"], W=[BK(0)])
                if slab + 4 < 8:
                    load_wada(l, slab + 4)
            do("dve", "tensor_tensor", out=modT[:, l], in0=modps,
               in1=b_adaT_sb[:, l, :].unsqueeze(2).to_broadcast([128, 48, 4]), op=ALU.add,
               R=["b_adaT_sb"], W=[BK(0), ("modT", l)])
            for (kind, seg, gi, plus1) in ((0, 1, 0, True), (1, 2, 2, False), (2, 4, 1, True), (3, 5, 3, False)):
                src = modT[:, l, seg * 8:(seg + 1) * 8, :]
                gb = gT_sb[:, l, gi, :].unsqueeze(2).to_broadcast([128, 8, 4])
                if plus1:
                    do("dve", "tensor_scalar", out=tmp84, in0=src, scalar1=1.0, scalar2=None, op0=ALU.add,
                       R=[("modT", l)], W=["tmp84"])
                    do("dve", "tensor_tensor", out=cols[:, l, kind], in0=tmp84, in1=gb, op=ALU.mult,
                       R=["tmp84", "gT_sb"], W=[("cols", l)])
                else:
                    do("dve", "tensor_tensor", out=cols[:, l, kind], in0=src, in1=gb, op=ALU.mult,
                       R=[("modT", l), "gT_sb"], W=[("cols", l)])
            for v, (gsel, wsel) in enumerate(((0, 0), (1, 1), (0, 2), (1, 3))):
                do("dve", "tensor_tensor", out=tab_sb[:, :, v, :], in0=rope_gen[:, :, gsel, :],
                   in1=qk_b[:, l, wsel, :].unsqueeze(1).to_broadcast([128, 17, 64]), op=ALU.mult,
                   R=["rope_gen", "qk_b"], W=["tab_sb"])
            do("dve", "tensor_copy", out=tab_sb[:, :, 4:6, :], in_=rope_gen, R=["rope_gen"], W=["tab_sb"])
            do("sp", "dma_start", out=tabs_s[l].rearrange("t p f -> p t f"),
               in_=tab_sb.rearrange("p t v d -> p t (v d)"), R=["tab_sb"], W=[("tabs", l)], K=("tabs", l))

        mod_layer(0)
        for l in range(1, L):
            mod_layer(l)
        def late_convs():
            conv(w_up_s[0], w_up_d[0], D, ("cv_up", 0))
            conv(w_down_s[0], w_down_d[0], DFF, ("cv_down", 0))
            for l in range(1, L):
                conv(w_in_s[l], w_in_d[l], D, ("cv_in", l))
                conv(w_out_s[l], w_out_d[l], D, ("cv_out", l))
                conv(w_up_s[l], w_up_d[l], D, ("cv_up", l))
                conv(w_down_s[l], w_down_d[l], DFF, ("cv_down", l))

        if DEV.get("dump"):
            dbg("modT", modT, [128, L, 48, 4], F32, [("modT", l) for l in range(L)])
            dbg("cols", cols, [128, L, 4, 8, 4], F32, [("cols", l) for l in range(L)])
            dbg("sT", sT, [128, 8, 4], F32, ["sT"])
        A.reset(persist_mark)
        S.barrier(skip=lambda k: isinstance(k, tuple) and str(k[0]).startswith('cv_'))

        tab_ctr = [0]

        def load_tab(l, tsel):
            slot = tab_ctr[0] % 3
            tab_ctr[0] += 1
            do("sp", "dma_start", out=tabslot[:, slot].rearrange("p v d -> p (v d)"), in_=tabs_s[l, tsel],
               R=[("tabs", l)], W=[("tabslot", slot)], K=("tabslot", slot))
            return slot

        def rms_rstd(src_ap, n, ss_ap, out_ap, junk, res_src, res_out, banks=()):
            do("act", "activation", out=junk, in_=src_ap, func=AF.Square, accum_out=ss_ap,
               R=res_src, W=["junk", (res_out, "ss")] + [BK(x) for x in banks])
            do("act", "activation", out=ss_ap, in_=ss_ap, func=AF.Ln, scale=1.0 / n, bias=EPS,
               R=[(res_out, "ss")], W=[(res_out, "ss")])
            do("act", "activation", out=out_ap, in_=ss_ap, func=AF.Exp, scale=-0.5, R=[(res_out, "ss")], W=[res_out])

        for b in range(BPC):
            do("sp", "dma_start", out=h_sb, in_=x_in[b].rearrange("(t p) d -> p t d", p=128),
               W=[("h", t) for t in range(NT)], K="hload")
            do("sp", "dma_start", out=hc_sb, in_=ctx_in[b].rearrange("(t p) d -> p t d", p=128),
               W=[("hc", t) for t in range(NCT)], K="hload")

            for l in range(L):
                last = is_last[l]
                m0 = A.mark()

                def hres(tt):
                    return ("hc", tt) if tt < NCT else ("h", tt - NCT)

                def hsrc(tt):
                    return hc_sb[:, tt, :] if tt < NCT else h_sb[:, tt - NCT, :]

                def rsel(tt, b=b):
                    return 2 if tt < NCT else b

                def colap(kind, c, r, l=l):
                    return cols[:, l, kind, c, r:r + 1]

                def shap(seg, c, r, l=l):
                    return modT[:, l, seg * 8 + c, r:r + 1]

                diagb = A.alloc([128], F32)
                for gi, (kind, r) in enumerate(((1, b), (3, b), (1, 2), (3, 2))):
                    if r == 2 and last:
                        continue
                    for half in range(2):
                        for c4 in range(4):
                            c = half * 4 + c4
                            do("dve", "tensor_scalar", out=diagb, in0=ident_f, scalar1=colap(kind, c, r), scalar2=None,
                               op0=ALU.mult, R=["ident_f", ("cols", l)], W=["diagb"])
                            do("pe", "matmul", bank(0)[:, c4 * 128:(c4 + 1) * 128], lhsT=ones_f, rhs=diagb,
                               start=True, stop=True, R=["ones_f", "diagb"], W=[BK(0)])
                        do("act", "activation", out=Grow[:, gi, half * 512:(half + 1) * 512], in_=bank(0), func=AF.Copy,
                           W=[BK(0), ("Grow", gi)])

                kT_sb = A.alloc([2, NTT * 128], BF16)
                V_sb = A.alloc([NTT, 4, 66], BF16)
                w_kv = A.alloc([8, 512], BF16)
                w_q = A.alloc([8, 1024], BF16)
                w_o = A.alloc([8, 1024], BF16)
                hn2 = [A.alloc([D], BF16) for _ in range(3)]
                junk = A.alloc([D], BF16)
                uT2 = [A.alloc([8, 128], BF16) for _ in range(3)]
                sq_f = A.alloc([512], F32)
                t1 = A.alloc([16, 64], F32)
                t2 = A.alloc([16, 64], F32)
                q_tok = A.alloc([8, 2, 64], BF16)
                k_tok2 = [A.alloc([2, 2, 64], BF16) for _ in range(3)]
                qT2 = [A.alloc([16, 128], BF16) for _ in range(2)]
                NPT = 4
                PT = [A.alloc([512], BF16) for _ in range(NPT)]
                O_tok = A.alloc([16, 64], BF16)
                OT_sb = A.alloc([8, 128], BF16)
                tmpf = t1.rearrange("p h d -> p (h d)")
                ssm = A.alloc([8], F32)
                ss1 = A.alloc([NTT], F32)
                ssk2 = A.alloc([3, 2], F32)
                rstd_k2 = A.alloc([3, 2], F32)
                ssq = A.alloc([8], F32)
                rstd_q = A.alloc([8], F32)
                den = A.alloc([4], F32)
                rden = A.alloc([4], F32)

                wsrc = w_in_s[l].rearrange("(c p) f -> p c f", p=128)
                for i, c0 in enumerate((512, 1280, 640, 1408)):
                    do("sp", "dma_start", out=w_kv[:, :, i * 128:(i + 1) * 128], in_=wsrc[:, :, c0:c0 + 128],
                       R=cv_res("cv_in", l, D), W=["w_kv"], K="w_kv")
                for i, c0 in enumerate((0, 768)):
                    do("sp", "dma_start", out=w_q[:, :, i * 512:(i + 1) * 512], in_=wsrc[:, :, c0:c0 + 512],
                       R=cv_res("cv_in", l, D), W=["w_q"], K="w_q")
                do("sp", "dma_start", out=w_o, in_=w_out_s[l].rearrange("(c p) f -> p c f", p=128),
                   R=cv_res("cv_out", l, D), W=["w_o"], K="w_o")
                if b == 0 and l == 0:
                    S.op("pool", None, reads=["w_kv", "w_q", "w_o"] + [("h", t) for t in range(NT)] + [("hc", t) for t in range(NCT)])
                    late_convs()
                do("dve", "memset", V_sb, 1.0, W=["V_sb"])
                do("dve", "memset", qT2[0], 0.0, W=[("qT", 0)])
                do("dve", "memset", qT2[1], 0.0, W=[("qT", 1)])

                def front_a(tt, rstd_ap, rstd_res, have_rstd, tb, hn_ap, hn_res, junk_ap, ss_ap, hn_on_act=False):
                    if not have_rstd:
                        rms_rstd(hsrc(tt), D, ss_ap, rstd_ap, junk_ap, [hres(tt)], rstd_res)
                    if hn_on_act:
                        do("act", "activation", out=hn_ap, in_=hsrc(tt), func=AF.Copy, scale=rstd_ap,
                           R=[hres(tt), rstd_res], W=[hn_res])
                    else:
                        do("dve", "tensor_scalar", out=hn_ap, in0=hsrc(tt), scalar1=rstd_ap, scalar2=None, op0=ALU.mult,
                           R=[hres(tt), rstd_res], W=[hn_res])
                    tbv = bank_bf(tb)
                    for c in range(8):
                        do("pe", "transpose", out=tbv[:, c, :], in_=hn_ap[:, c * 128:(c + 1) * 128], identity=ident_bf,
                           R=[hn_res, "ident_bf"], W=[BK(tb)])

                def front_b(tt, acol_kind, sh_seg, tb, dst, dst_res):
                    r = rsel(tt)
                    tbv = bank_bf(tb)
                    for c in range(8):
                        do("dve", "tensor_scalar", out=dst[:, c, :], in0=tbv[:, c, :], scalar1=colap(acol_kind, c, r),
                           scalar2=shap(sh_seg, c, r), op0=ALU.mult, op1=ALU.add,
                           R=[("cols", l), ("modT", l)], W=[BK(tb), dst_res])

                def rope(src_ps, nh, slot, vC, vS, out_ap, rstd_bc, rstd_res, banks, out_res):
                    C = tabslot[:, slot, vC, :]
                    Sg = tabslot[:, slot, vS, :].rearrange("p (a b d) -> p a b d", a=2, b=2)
                    bw = [BK(x) for x in banks]
                    t1v = t1[:, 0:nh, :]
                    t2f = t2[:, 0:nh, :]
                    t2v = t2f.rearrange("p h (a b d) -> p h a b d", a=2, b=2)
                    s5 = src_ps.rearrange("p h (a b d) -> p h a b d", a=2, b=2)
                    do("dve", "tensor_tensor", out=t1v, in0=src_ps, in1=C.unsqueeze(1).to_broadcast([128, nh, 64]), op=ALU.mult,
                       R=[("tabslot", slot)], W=bw + ["t1"])
                    do("dve", "tensor_tensor", out=t2v[:, :, :, 0, :], in0=s5[:, :, :, 1, :],
                       in1=Sg[:, :, 0, :].unsqueeze(1).to_broadcast([128, nh, 2, 16]), op=ALU.mult,
                       R=[("tabslot", slot)], W=bw + ["t2"])
                    do("dve", "tensor_tensor", out=t2v[:, :, :, 1, :], in0=s5[:, :, :, 0, :],
                       in1=Sg[:, :, 1, :].unsqueeze(1).to_broadcast([128, nh, 2, 16]), op=ALU.mult,
                       R=[("tabslot", slot)], W=bw + ["t2"])
                    if rstd_bc is None:
                        do("dve", "tensor_tensor", out=out_ap, in0=t1v, in1=t2f, op=ALU.add, R=["t1", "t2"], W=[out_res])
                    else:
                        do("dve", "tensor_tensor", out=t1v, in0=t1v, in1=t2f, op=ALU.add, R=["t1", "t2"], W=["t1"])
                        do("dve", "tensor_tensor", out=out_ap, in0=t1v, in1=rstd_bc, op=ALU.mult, R=["t1", rstd_res], W=[out_res])

                def head_rstd_a(src_ps, nh, ss_ap, ss_res, banks):
                    bw = [BK(x) for x in banks]
                    do("act", "activation", out=sq_f[:, 0:nh * 64], in_=src_ps, func=AF.Square, W=bw + ["sq_f"])
                    do("dve", "tensor_reduce", out=ss_ap, in_=sq_f[:, 0:nh * 64].rearrange("p (h d) -> p h d", d=64),
                       axis=AX.X, op=ALU.add, R=["sq_f"], W=[ss_res])

                def head_rstd_b(ss_ap, ss_res, out_ap, out_res):
                    do("act", "activation", out=ss_ap, in_=ss_ap, func=AF.Ln, scale=1.0 / 64, bias=EPS, R=[ss_res], W=[ss_res])
                    do("act", "activation", out=out_ap, in_=ss_ap, func=AF.Exp, scale=-0.5, R=[ss_res], W=[out_res])

                slots1 = {}

                def p1R(tt):
                    rms_rstd(hsrc(tt), D, ss1[:, tt:tt + 1], rstd1[:, tt:tt + 1], junk, [hres(tt)], ("rstd1", tt))

                for tt in range(4):
                    p1R(tt)

                TB3 = [0, 1, 6]
                KV3 = [2, 3, 7]

                def p1A(tt):
                    pb = tt % 3
                    slots1[tt] = load_tab(l, 16 if tt < NCT else tt - NCT)
                    front_a(tt, rstd1[:, tt:tt + 1], ("rstd1", tt), True, TB3[pb], hn2[pb], ("hn", pb), junk, None, hn_on_act=True)
                    front_b(tt, 0, 0, TB3[pb], uT2[pb], ("uT", pb))

                sqk = [A.alloc([128], F32) for _ in range(3)]

                def p1B1(tt):
                    pb = tt % 3
                    kvb = KV3[pb]
                    uT = uT2[pb]
                    for c in range(8):
                        do("pe", "matmul", bank(kvb), lhsT=uT[:, c, :], rhs=w_kv[:, c, :], start=(c == 0), stop=(c == 7),
                           R=[("uT", pb), "w_kv"], W=[BK(kvb)])
                    kv = bank(kvb)
                    do("act", "activation", out=sqk[pb], in_=kv[:, 0:128], func=AF.Square, W=[BK(kvb), ("sqk", pb)])
                    do("act", "activation", out=V_sb[:, tt, :, 0:64], in_=kv[:, 256:512].rearrange("p (h d) -> p h d", d=64),
                       func=AF.Copy, W=[BK(kvb), "V_sb"])
                    do("dve", "tensor_reduce", out=ssk2[:, pb], in_=sqk[pb].rearrange("p (h d) -> p h d", d=64),
                       axis=AX.X, op=ALU.add, R=[("sqk", pb)], W=[("ssk", pb)])

                def p1B2(tt):
                    pb = tt % 3
                    kvb = KV3[pb]
                    slot = slots1[tt]
                    k_tok = k_tok2[pb]
                    kv = bank(kvb)
                    head_rstd_b(ssk2[:, pb], ("ssk", pb), rstd_k2[:, pb], ("rstd_k", pb))
                    rope(kv[:, 128:256].rearrange("p (h d) -> p h d", d=64), 2, slot, 4, 5, k_tok[:, :, 1, :], None, None, [kvb], ("k_tok", pb))
                    rope(kv[:, 0:128].rearrange("p (h d) -> p h d", d=64), 2, slot, 2, 3, k_tok[:, :, 0, :],
                         rstd_k2[:, pb].unsqueeze(2).to_broadcast([128, 2, 64]), ("rstd_k", pb), [kvb], ("k_tok", pb))

                def p1B3(tt):
                    pb = tt % 3
                    ktbank = 4 + tt % 2
                    k_tok = k_tok2[pb]
                    ktb = bank_bf(ktbank)
                    for j in range(2):
                        do("pe", "transpose", out=ktb[:, j, :], in_=k_tok[:, j].rearrange("p a d -> p (a d)"), identity=ident_bf,
                           R=[("k_tok", pb), "ident_bf"], W=[BK(ktbank)])
                    do("act", "activation", out=kT_sb[:, :, tt * 128:(tt + 1) * 128], in_=ktb[:, 0:2, :], func=AF.Copy,
                       W=[BK(ktbank), "kT_sb"])

                for it in range(-3, NTT):
                    if 4 <= it + 5 < NTT:
                        p1R(it + 5)
                    if 0 <= it + 3 < NTT:
                        p1A(it + 3)
                    if 0 <= it + 2 < NTT:
                        p1B1(it + 2)
                    if 0 <= it + 1 < NTT:
                        p1B2(it + 1)
                    if 0 <= it < NTT:
                        p1B3(it)

                if DEV.get("dump") and b == 0 and l == 0:
                    dbg("rstd1", rstd1, [128, NTT], F32, [("rstd1", t) for t in range(NTT)])
                    dbg("kT", kT_sb, [128, 2, NTT * 128], BF16, ["kT_sb"])
                    dbg("V", V_sb, [128, NTT, 4, 66], BF16, ["V_sb"])

                q_tiles = list(range(NCT, NTT)) + ([] if last else list(range(NCT)))
                if DEV.get("max_q_tiles") is not None:
                    q_tiles = q_tiles[:DEV["max_q_tiles"]]
                SBANKS = [3, 4, 5]
                TBANK = 0
                tabs_of = {}
                uTq = uT2[0]

                def prepA(tt):
                    tabs_of[tt] = load_tab(l, 16 if tt < NCT else tt - NCT)
                    front_a(tt, rstd1[:, tt:tt + 1], ("rstd1", tt), True, TBANK, hn2[0], ("hn", 0), junk, None)
                    front_b(tt, 0, 0, TBANK, uTq, ("uT", 0))

                def prepB(tt):
                    for n in range(2):
                        for c in range(8):
                            do("pe", "matmul", bank(1 + n), lhsT=uTq[:, c, :], rhs=w_q[:, c, n * 512:(n + 1) * 512],
                               start=(c == 0), stop=(c == 7), R=[("uT", 0), "w_q"], W=[BK(1 + n)])

                def prepC(tt):
                    head_rstd_a(bank(1), 8, ssq, "ssq", [1])

                def prepD(tt):
                    slot = tabs_of[tt]
                    head_rstd_b(ssq, "ssq", rstd_q, "rstd_q")
                    rope(bank(1).rearrange("p (h d) -> p h d", d=64), 8, slot, 0, 1, q_tok[:, :, 0, :],
                         rstd_q.unsqueeze(2).to_broadcast([128, 8, 64]), "rstd_q", [1], "q_tok")
                    rope(bank(2).rearrange("p (h d) -> p h d", d=64), 8, slot, 4, 5, q_tok[:, :, 1, :], None, None, [2], "q_tok")

                def prepE(tt):
                    qtb = bank_bf(TBANK)
                    for j in range(8):
                        do("pe", "transpose", out=qtb[:, j, :], in_=q_tok[:, j].rearrange("p a d -> p (a d)"), identity=ident_bf,
                           R=["q_tok", "ident_bf"], W=[BK(TBANK)])

                def prepF(tt, qb):
                    qtb = bank_bf(TBANK)
                    do("dve", "tensor_copy", out=qT2[qb][0:64, 0:8, :], in_=qtb[0:64, :, :], W=[BK(TBANK), ("qT", qb)])
                    do("dve", "tensor_copy", out=qT2[qb][64:128, 8:16, :], in_=qtb[64:128, :, :], W=[BK(TBANK), ("qT", qb)])

                def tailA(tt):
                    otb = bank_bf(TBANK)
                    Of = O_tok.rearrange("p h d -> p (h d)")
                    for c in range(8):
                        do("pe", "transpose", out=otb[:, c, :], in_=Of[:, c * 128:(c + 1) * 128], identity=ident_bf,
                           R=["O_tok", "ident_bf"], W=[BK(TBANK)])
                    do("dve", "tensor_copy", out=OT_sb, in_=otb, W=[BK(TBANK), "OT_sb"])

                def tailB(tt):
                    for n in range(2):
                        for c in range(8):
                            do("pe", "matmul", bank(1 + n), lhsT=OT_sb[:, c, :], rhs=w_o[:, c, n * 512:(n + 1) * 512],
                               start=(c == 0), stop=(c == 7), R=["OT_sb", "w_o"], W=[BK(1 + n)])

                def tailC(tt):
                    mix = ps[:, 1:3, :].rearrange("p a b -> p (a b)")
                    rms_rstd(mix, D, ssm[:, 0:1], ssm[:, 1:2], junk, [], "rstd_m", banks=(1, 2))
                    gsel = 2 if tt < NCT else 0
                    do("dve", "scalar_tensor_tensor", out=tmpf, in0=mix, scalar=ssm[:, 1:2], in1=Grow[:, gsel, :],
                       op0=ALU.mult, op1=ALU.mult, R=["rstd_m", ("Grow", gsel)], W=[BK(1), BK(2), "tmpf"])
                    do("dve", "tensor_tensor", out=hsrc(tt), in0=hsrc(tt), in1=tmpf, op=ALU.add,
                       R=["tmpf", hres(tt)], W=[hres(tt)])

                def make_steps(tt):
                    is_ctx = tt < NCT
                    i_lat = tt - NCT
                    steps = []
                    for g in range(4):
                        ab, kvh = g // 2, g % 2
                        if is_ctx:
                            kbs = [(0, None), (1, None)]
                        elif ab == 0:
                            kbs = [(kb, None) for kb in range(NTT)]
                        else:
                            kbs = []
                            if i_lat > 0:
                                kbs.append((NCT + i_lat - 1, 0))
                            kbs.append((NCT + i_lat, None))
                            if i_lat < NT - 1:
                                kbs.append((NCT + i_lat + 1, 1))
                            kbs += [(0, None), (1, None)]
                        for ki, (kb, m) in enumerate(kbs):
                            steps.append(dict(g=g, ab=ab, kvh=kvh, kb=kb, m=m, first=(ki == 0), lastk=(ki == len(kbs) - 1)))
                    return steps

                sctr = [0]

                def attn(tt, qb, hooks):
                    steps = make_steps(tt)
                    ns = len(steps)
                    base = sctr[0]
                    sctr[0] += ns
                    qT_sb = qT2[qb]

                    def emit_S(si):
                        s = steps[si]
                        sb_ = SBANKS[(base + si) % 3]
                        h0 = s['ab'] * 8 + s['kvh'] * 4
                        rhs = qT_sb[:, h0:h0 + 4, :].rearrange("p h q -> p (h q)")
                        do("pe", "matmul", bank(sb_), lhsT=kT_sb[:, s['kvh'], s['kb'] * 128:(s['kb'] + 1) * 128], rhs=rhs,
                           start=True, stop=(s['m'] is None), R=["kT_sb", ("qT", qb)], W=[BK(sb_)])
                        if s['m'] is not None:
                            do("pe", "matmul", bank(sb_), lhsT=ident_bf, rhs=mask_bf[:, s['m'], :], start=False, stop=True,
                               R=["ident_bf", "mask_bf"], W=[BK(sb_)])

                    def emit_EXP_PV(si):
                        s = steps[si]
                        sb_ = SBANKS[(base + si) % 3]
                        pslot = (base + si) % NPT
                        ob = 6 + s['g'] % 2
                        obv = bank(ob).rearrange("p (h x) -> p h x", h=4)
                        do("act", "activation", out=PT[pslot], in_=bank(sb_), func=AF.Exp, scale=0.125,
                           W=[BK(sb_), ("PT", pslot)])
                        for hh in range(4):
                            do("pe", "matmul", obv[:, hh, 0:65], lhsT=PT[pslot][:, hh * 128:(hh + 1) * 128],
                               rhs=V_sb[:, s['kb'], s['ab'] * 2 + s['kvh'], 0:65],
                               start=(s['first'] and hh == 0), stop=s['lastk'], skip_group_check=True,
                               R=[("PT", pslot), "V_sb"], W=[BK(ob)])
                        if s['lastk']:
                            g = s['g']
                            if s['ab'] == 1:
                                do("dve", "tensor_tensor", out=den, in0=obv[:, :, 64], in1=sinkexp[:, l, s['kvh'] * 4:(s['kvh'] + 1) * 4],
                                   op=ALU.add, R=["sinkexp"], W=[BK(ob), "den"])
                                do("dve", "reciprocal", out=rden, in_=den, R=["den"], W=["rden"])
                            else:
                                do("dve", "reciprocal", out=rden, in_=obv[:, :, 64], W=[BK(ob), "rden"])
                            do("dve", "tensor_tensor", out=O_tok[:, g * 4:(g + 1) * 4, :], in0=obv[:, :, 0:64],
                               in1=rden.unsqueeze(2).to_broadcast([128, 4, 64]), op=ALU.mult,
                               R=["rden"], W=[BK(ob), "O_tok"])

                    for si in range(min(3, ns)):
                        emit_S(si)
                    for si in range(ns):
                        emit_EXP_PV(si)
                        if si + 3 < ns:
                            emit_S(si + 3)
                        for f in hooks.get(si, ()):
                            f()

                HK = DEV.get("hooks", dict(tA=0, tB=2, tC=5, pA=7, pB=12, pC=18, pD=20, pE=34, pF=37))
                if q_tiles:
                    t0_ = q_tiles[0]
                    prepA(t0_); prepB(t0_); prepC(t0_); prepD(t0_); prepE(t0_); prepF(t0_, 0)
                for idx, tt in enumerate(q_tiles):
                    ns_ = len(make_steps(tt))
                    hooks = {}

                    def at(step, f, hooks=hooks, ns_=ns_):
                        hooks.setdefault(min(step, ns_ - 1), []).append(f)

                    if idx > 0:
                        prev = q_tiles[idx - 1]
                        at(HK['tA'], lambda prev=prev: tailA(prev))
                        at(HK['tB'], lambda prev=prev: tailB(prev))
                        at(HK['tC'], lambda prev=prev: tailC(prev))
                    if idx + 1 < len(q_tiles):
                        nxt = q_tiles[idx + 1]
                        nqb = (idx + 1) % 2
                        at(HK['pA'], lambda nxt=nxt: prepA(nxt))
                        at(HK['pB'], lambda nxt=nxt: prepB(nxt))
                        at(HK['pC'], lambda nxt=nxt: prepC(nxt))
                        at(HK['pD'], lambda nxt=nxt: prepD(nxt))
                        at(HK['pE'], lambda nxt=nxt: prepE(nxt))
                        at(HK['pF'], lambda nxt=nxt, nqb=nqb: prepF(nxt, nqb))
                    attn(tt, idx % 2, hooks)
                if q_tiles:
                    tailA(q_tiles[-1]); tailB(q_tiles[-1]); tailC(q_tiles[-1])

                if DEV.get("dump") and b == 0 and l == 0:
                    dbg("h_p2", h_sb, [128, NT, D], F32, [("h", t) for t in range(NT)])
                    dbg("hc_p2", hc_sb, [128, NCT, D], F32, [("hc", t) for t in range(NCT)])

                A.reset(m0)
                S.barrier(skip=lambda k: isinstance(k, tuple) and str(k[0]).startswith('cv_'))
                if not DEV.get("skip_mlp"):
                    mlp_tiles = list(range(NCT, NTT)) + ([] if last else list(range(NCT)))
                    ntl = len(mlp_tiles)
                    u2T = A.alloc([8, NTT * 128], BF16)
                    wu_buf = [A.alloc([8, 512], BF16) for _ in range(3)]
                    wd_buf = [A.alloc([4, D], BF16) for _ in range(3)]
                    hr_f = [A.alloc([384], F32) for _ in range(2)]
                    hT = [A.alloc([384], BF16) for _ in range(2)]
                    hn3 = A.alloc([D], BF16)
                    junk3 = A.alloc([D], BF16)
                    tmp3 = A.alloc([D], F32)
                    ssy = A.alloc([8], F32)
                    hn3 = [hn3, A.alloc([D], BF16)]
                    ss3 = A.alloc([NTT], F32)
                    def prep3(ti):
                        tt = mlp_tiles[ti]
                        tb3 = 6 + ti % 2
                        front_a(tt, rstd2[:, tt:tt + 1], ("rstd2", tt), False, tb3, hn3[ti % 2], ("hn", ti % 2), junk3, ss3[:, tt:tt + 1], hn_on_act=True)
                        front_b(tt, 2, 3, tb3, u2T[:, :, ti * 128:(ti + 1) * 128], ("u2T", ti))

                    if ntl == 18:
                        bsz = [2, 3, 3, 3, 3, 2, 2]
                    else:
                        bsz = [2, 3, 3, 3, 3, 2]
                    blocks = []
                    p0 = 0
                    for n_ in bsz:
                        blocks.append(list(range(p0, p0 + n_)))
                        p0 += n_
                    if DEV.get("max_blocks") is not None:
                        blocks = blocks[:DEV["max_blocks"]]
                    prep3(0)
                    prep3(1)
                    prep_hooks = {}
                    for k_ in range(2, ntl):
                        prep_hooks[2 * (k_ - 2) + 1] = k_
                    wctr = [0]
                    for blk in blocks:
                        nt_b = len(blk)
                        ntok = nt_b * 128
                        tok0 = blk[0] * 128
                        wslots = {}

                        def emit_U(f):
                            fg, fl = f // 4, f % 4
                            if fl == 0:
                                wslot = wctr[0] % 3
                                wctr[0] += 1
                                do("sp", "dma_start", out=wu_buf[wslot],
                                   in_=w_up_s[l][:, fg * 512:(fg + 1) * 512].rearrange("(c p) f -> p c f", p=128),
                                   R=cv_res("cv_up", l, D), W=[("wu", wslot)], K=("wu", wslot))
                                do("sp", "dma_start", out=wd_buf[wslot],
                                   in_=w_down_s[l][fg * 512:(fg + 1) * 512, :].rearrange("(fl p) n -> p fl n", p=128),
                                   R=cv_res("cv_down", l, DFF), W=[("wd", wslot)], K=("wd", wslot))
                                wslots[fg] = wslot
                            ws = wslots[fg]
                            ub = f % 2
                            for c in range(8):
                                do("pe", "matmul", bank(ub)[:, 0:ntok], lhsT=wu_buf[ws][:, c, fl * 128:(fl + 1) * 128],
                                   rhs=u2T[:, c, tok0:tok0 + ntok], start=(c == 0), stop=(c == 7),
                                   R=[("wu", ws)] + [("u2T", ti) for ti in blk], W=[BK(ub)])
                            do("act", "activation", out=hr_f[ub][:, 0:ntok], in_=bank(ub)[:, 0:ntok], func=AF.Relu,
                               W=[BK(ub), ("hr", ub)])
                            do("act", "activation", out=hT[ub][:, 0:ntok], in_=hr_f[ub][:, 0:ntok], func=AF.Square,
                               R=[("hr", ub)], W=[("hT", ub)])

                        def emit_D(f):
                            fg, fl = f // 4, f % 4
                            ws = wslots[fg]
                            ub = f % 2
                            for t in range(nt_b):
                                for n in range(2):
                                    yb = 2 + t * 2 + n
                                    do("pe", "matmul", bank(yb), lhsT=hT[ub][:, t * 128:(t + 1) * 128],
                                       rhs=wd_buf[ws][:, fl, n * 512:(n + 1) * 512], start=(f == 0), stop=(f == 31),
                                       R=[("hT", ub), ("wd", ws)], W=[BK(yb)])

                        emit_U(0)
                        for f in range(32):
                            if f + 1 < 32:
                                emit_U(f + 1)
                            emit_D(f)
                            if blk is blocks[0] and f in prep_hooks:
                                prep3(prep_hooks[f])
                        for t, ti in enumerate(blk):
                            tt = mlp_tiles[ti]
                            y = ps[:, 2 + 2 * t:4 + 2 * t, :].rearrange("p a b -> p (a b)")
                            rms_rstd(y, D, ssy[:, 0:1], ssy[:, 1:2], junk3, [], "rstd_y", banks=(2 + 2 * t, 3 + 2 * t))
                            gsel = 3 if tt < NCT else 1
                            do("dve", "scalar_tensor_tensor", out=tmp3, in0=y, scalar=ssy[:, 1:2], in1=Grow[:, gsel, :],
                               op0=ALU.mult, op1=ALU.mult, R=["rstd_y", ("Grow", gsel)],
                               W=[BK(2 + 2 * t), BK(3 + 2 * t), "tmpf"])
                            do("dve", "tensor_tensor", out=hsrc(tt), in0=hsrc(tt), in1=tmp3, op=ALU.add,
                               R=["tmpf", hres(tt)], W=[hres(tt)])
                A.reset(m0)
                S.barrier(skip=lambda k: isinstance(k, tuple) and str(k[0]).startswith('cv_'))

            do("sp", "dma_start", out=h_out[b].rearrange("(t p) d -> p t d", p=128), in_=h_sb,
               R=[("h", t) for t in range(NT)], K="out")
            if emit_hc:
                do("sp", "dma_start", out=hc_out[b].rearrange("(t p) d -> p t d", p=128), in_=hc_sb,
                   R=[("hc", t) for t in range(NCT)], K="out")

        S.wait_dma_all("sp", "out")
        if dbg_outs:
            S.wait_dma_all("sp", "dbg")
        S.emit()
        build_program.stats = S.stats
    return nc, dbg_outs


def _rope_tables():
    theta = np.float32(10000.0)
    axis_dim = 32
    inv = (theta ** (-(np.arange(0, axis_dim, 2, dtype=np.float32) / np.float32(axis_dim)))).astype(np.float32)
    tok = np.arange(SEQ)
    row = (tok // 64).astype(np.float32)
    col = (tok % 64).astype(np.float32)
    ang_r = row[:, None] * inv[None, :]
    ang_c = col[:, None] * inv[None, :]
    cr, sr, cc, sc = np.cos(ang_r), np.sin(ang_r), np.cos(ang_c), np.sin(ang_c)
    C = np.concatenate([cr, cr, cc, cc], axis=1).astype(np.float32)
    Sg = np.concatenate([-sr, sr, -sc, sc], axis=1).astype(np.float32)
    out = np.zeros((128, 17, 2, 64), np.float32)
    out[:, :16, 0, :] = C.reshape(16, 128, 64).transpose(1, 0, 2)
    out[:, :16, 1, :] = Sg.reshape(16, 128, 64).transpose(1, 0, 2)
    out[:, 16, 0, :] = 1.0
    return out


def _masks():
    kk = np.arange(128)[:, None]
    r = np.arange(128)[None, :]
    mprev = np.where(kk >= r, 0.0, NEG).astype(np.float32)
    mnext = np.where(kk <= r, 0.0, NEG).astype(np.float32)
    m = np.zeros((128, 2, 512), np.float32)
    m[:, 0, :] = np.tile(mprev, (1, 4))
    m[:, 1, :] = np.tile(mnext, (1, 4))
    return m


def _swap16(v):
    v4 = v.reshape(v.shape[:-1] + (2, 2, 16))
    return np.ascontiguousarray(v4[..., ::-1, :]).reshape(v.shape)


def _common_inputs(ls, c, c_ctx, w_ada, b_ada, g_pre_mix, g_post_mix, g_pre_mlp, g_post_mlp,
                   w_in, q_norm, k_norm, sink, w_out, w_up, w_down):
    L = len(ls)
    f = lambda a: np.ascontiguousarray(np.asarray(a, dtype=np.float32))
    sl = slice(ls[0], ls[-1] + 1)
    b_adaT = f(np.asarray(b_ada)[sl].reshape(L, 48, 128).transpose(2, 0, 1))
    g4 = np.stack([np.asarray(g_pre_mix)[sl], np.asarray(g_pre_mlp)[sl], np.asarray(g_post_mix)[sl], np.asarray(g_post_mlp)[sl]], axis=1)
    gT = f(g4.reshape(L, 4, 8, 128).transpose(3, 0, 1, 2))
    qn = np.asarray(q_norm)[sl]
    kn = np.asarray(k_norm)[sl]
    qk = f(np.stack([qn, _swap16(qn), kn, _swap16(kn)], axis=1).reshape(-1))
    common = {
        "w_ada": f(np.asarray(w_ada)[sl]), "b_adaT": b_adaT, "gT": gT, "qk_norm": qk,
        "sink": f(np.asarray(sink)[sl].reshape(-1)),
        "w_in": f(np.asarray(w_in)[sl]), "w_out": f(np.asarray(w_out)[sl]),
        "w_up": f(np.asarray(w_up)[sl]), "w_down": f(np.asarray(w_down)[sl]),
        "ident": np.eye(128, dtype=np.float32), "rope_cs": _rope_tables(), "masks": _masks(),
    }
    cTs = []
    c = np.asarray(c, dtype=np.float32)
    c_ctx = np.asarray(c_ctx, dtype=np.float32)
    for core in range(NCORES):
        rows = np.zeros((4, D), np.float32)
        rows[0] = c[core * BPC]
        rows[1] = c[core * BPC + 1]
        rows[2] = c_ctx
        cTs.append(f(rows.reshape(4, 8, 128).transpose(2, 1, 0)))
    return common, cTs


_PROG_CACHE = {}


def _get_prog(L, is_last, emit_hc):
    key = (L, tuple(is_last), emit_hc, repr(sorted(DEV.items())))
    if key not in _PROG_CACHE:
        _PROG_CACHE[key] = build_program(L, list(is_last), emit_hc)
    return _PROG_CACHE[key]


def _launch(ls, is_last, emit_hc, h, hc, kw):
    nc, dbg_names = _get_prog(len(ls), is_last, emit_hc)
    common, cTs = _common_inputs(ls, **kw)
    in_maps = []
    for core in range(NCORES):
        m = dict(common)
        m["x_in"] = np.ascontiguousarray(h[core * BPC:(core + 1) * BPC])
        m["ctx_in"] = np.ascontiguousarray(hc[core * BPC:(core + 1) * BPC])
        m["cT"] = cTs[core]
        in_maps.append(m)
    res = run_bass_kernel_spmd(nc, in_maps, core_ids=list(range(NCORES)))
    h_new = np.concatenate([r["h_out"] for r in res.results], axis=0)
    hc_new = None
    if emit_hc:
        hc_new = np.concatenate([r["hc_out"] for r in res.results], axis=0)
    return h_new, hc_new, res


FUSED = True


def kernel(x, c, ctx, c_ctx, w_ada, b_ada, g_pre_mix, g_post_mix, g_pre_mlp, g_post_mlp,
           w_in, q_norm, k_norm, sink, w_out, w_up, w_down):
    kw = dict(c=c, c_ctx=c_ctx, w_ada=w_ada, b_ada=b_ada, g_pre_mix=g_pre_mix, g_post_mix=g_post_mix,
              g_pre_mlp=g_pre_mlp, g_post_mlp=g_post_mlp, w_in=w_in, q_norm=q_norm, k_norm=k_norm,
              sink=sink, w_out=w_out, w_up=w_up, w_down=w_down)
    h = np.asarray(x, dtype=np.float32)
    hc = np.asarray(ctx, dtype=np.float32)
    if FUSED:
        h, _, _ = _launch([0, 1], [False, True], False, h, hc, kw)
    else:
        h, hc, _ = _launch([0], [False], True, h, hc, kw)
        h, _, _ = _launch([1], [True], False, h, hc, kw)
    return h.astype(np.float32)
```

```python
import contextlib
import numpy as np
import concourse.bass as bass
import concourse.mybir as mybir
from concourse.bass_utils import run_bass_kernel_spmd

F32 = mybir.dt.float32
BF16 = mybir.dt.bfloat16
U8 = mybir.dt.uint8
ALU = mybir.AluOpType
AF = mybir.ActivationFunctionType
AX = mybir.AxisListType

D = 1024
SEQ = 2048
CTX = 256
NT = 16
NCT = 2
NTT = NT + NCT
DFF = 4096
EPS = 1e-6
NEG = -30000.0
NCORES = 8
BPC = 2

DEV = {}

ENGS = ["pe", "act", "dve", "pool", "sp"]


class Sched:
    def __init__(self, nc):
        self.nc = nc
        self.ops = {e: [] for e in ENGS}
        self.last_w = {}
        self.readers = {}
        self.dma_cnt = {}
        self.dma_keys = []
        self.waited = {e: {} for e in ENGS}

    def _add_dep(self, eng, deps, tok, same_ok):
        if tok is None:
            return
        if tok[0] == 'e' and tok[1] == eng and (not same_ok or eng == 'pe'):
            return
        src = (tok[0], tok[1])
        val = tok[2]
        if tok[0] == 'd':
            val = self.dma_cnt[tok[1]]
        if val <= self.waited[eng].get(src, -1):
            return
        if val > deps.get(src, -1):
            deps[src] = val

    def op(self, eng, fn, reads=(), writes=(), dma_key=None):
        deps = {}
        for r in reads:
            self._add_dep(eng, deps, self.last_w.get(r), True)
        for r in writes:
            self._add_dep(eng, deps, self.last_w.get(r), False)
            for t in self.readers.get(r, ()):
                self._add_dep(eng, deps, t, False)
        for src, v in deps.items():
            self.waited[eng][src] = v
        idx = len(self.ops[eng])
        if dma_key is not None:
            if dma_key not in self.dma_cnt:
                self.dma_cnt[dma_key] = 0
                self.dma_keys.append(dma_key)
            self.dma_cnt[dma_key] += 1
            tok = ('d', dma_key, self.dma_cnt[dma_key])
        else:
            tok = ('e', eng, idx)
        self.ops[eng].append(dict(fn=fn, deps=deps, dma_key=dma_key, signal=False))
        if fn is None:
            return None
        for r in reads:
            self.readers.setdefault(r, []).append(tok)
        for r in writes:
            self.last_w[r] = tok
            self.readers[r] = []
        return tok

    def barrier(self, skip=lambda k: False):
        last = {}
        for e in ENGS:
            for i in range(len(self.ops[e]) - 1, -1, -1):
                o = self.ops[e][i]
                if o['fn'] is not None and o['dma_key'] is None:
                    last[e] = i
                    break
        for e in ENGS:
            deps = {}
            for src, i in last.items():
                if src != e and i > self.waited[e].get(('e', src), -1):
                    deps[('e', src)] = i
            for k, c in self.dma_cnt.items():
                if skip(k):
                    continue
                if c > self.waited[e].get(('d', k), -1):
                    deps[('d', k)] = c
            for src, v in deps.items():
                self.waited[e][src] = v
            self.ops[e].append(dict(fn=None, deps=deps, dma_key=None, signal=False))

    def wait_dma_all(self, eng, key):
        self.ops[eng].append(dict(fn=None, deps={('d', key): self.dma_cnt[key]}, dma_key=None, signal=False))

    def emit(self):
        nc = self.nc
        for e in ENGS:
            for o in self.ops[e]:
                for (kind, src), v in o['deps'].items():
                    if kind == 'e':
                        self.ops[src][v]['signal'] = True
        rank = {e: {} for e in ENGS}
        for e in ENGS:
            c = 0
            for i, o in enumerate(self.ops[e]):
                if o['signal']:
                    assert o['dma_key'] is None and o['fn'] is not None
                    c += 1
                    rank[e][i] = c
        with contextlib.ExitStack() as st:
            esem = {e: st.enter_context(nc.semaphore("s_" + e)) for e in ENGS}
            dsem = {k: st.enter_context(nc.semaphore("d_%d" % i)) for i, k in enumerate(self.dma_keys)}
            block = st.enter_context(nc.Block())

            def body(e):
                def run(engine):
                    for i, o in enumerate(self.ops[e]):
                        for (kind, src), v in o['deps'].items():
                            if kind == 'e':
                                engine.wait_ge(esem[src], rank[src][v])
                            else:
                                engine.wait_ge(dsem[src], 16 * v)
                        if o['fn'] is None:
                            continue
                        ins = o['fn'](engine)
                        if o['dma_key'] is not None:
                            ins.then_inc(dsem[o['dma_key']], 16)
                        elif o['signal']:
                            ins.then_inc(esem[e], 1)
                return run

            block.tensor(body("pe"))
            block.scalar(body("act"))
            block.vector(body("dve"))
            block.gpsimd(body("pool"))
            block.sync(body("sp"))
        self.stats = {e: (len(self.ops[e]), len(rank[e])) for e in ENGS}


def _dtsize(dt):
    return {F32: 4, BF16: 2, U8: 1}[dt]


class Arena:
    def __init__(self, t, cap):
        self.t = t
        self.cap = cap
        self.off = 0

    def mark(self):
        return self.off

    def reset(self, m):
        self.off = m

    def alloc(self, free_shape, dt):
        n = int(np.prod(free_shape)) * _dtsize(dt)
        off = (self.off + 31) // 32 * 32
        assert off + n <= self.cap, ("SBUF arena overflow", off + n, self.cap)
        self.off = off + n
        ap = self.t[:, off:off + n].bitcast(dt)
        if len(free_shape) == 2:
            ap = ap.rearrange("p (a b) -> p a b", a=free_shape[0])
        elif len(free_shape) == 3:
            ap = ap.rearrange("p (a b c) -> p a b c", a=free_shape[0], b=free_shape[1])
        elif len(free_shape) == 4:
            ap = ap.rearrange("p (a b c d) -> p a b c d", a=free_shape[0], b=free_shape[1], c=free_shape[2])
        return ap


def build_program(L, is_last, emit_hc):
    nc = bass.Bass("TRN2", target_bir_lowering=False)

    def din(name, shape, dt=F32):
        return nc.dram_tensor(name, list(shape), dt, kind="ExternalInput").ap()

    x_in = din("x_in", [BPC, SEQ, D])
    ctx_in = din("ctx_in", [BPC, CTX, D])
    cT_d = din("cT", [128, 8, 4])
    w_ada_d = din("w_ada", [L, D, 6 * D])
    b_adaT_d = din("b_adaT", [128, L, 48])
    gT_d = din("gT", [128, L, 4, 8])
    qk_d = din("qk_norm", [L * 4 * 64])
    sink_d = din("sink", [L * 8])
    w_in_d = din("w_in", [L, D, 1536])
    w_out_d = din("w_out", [L, D, D])
    w_up_d = din("w_up", [L, D, DFF])
    w_down_d = din("w_down", [L, DFF, D])
    ident_d = din("ident", [128, 128])
    rope_d = din("rope_cs", [128, 17, 2, 64])
    masks_d = din("masks", [128, 2, 512])
    h_out = nc.dram_tensor("h_out", [BPC, SEQ, D], F32, kind="ExternalOutput").ap()
    hc_out = None
    if emit_hc:
        hc_out = nc.dram_tensor("hc_out", [BPC, CTX, D], F32, kind="ExternalOutput").ap()

    def dscr(name, shape, dt):
        return nc.dram_tensor(name, list(shape), dt, kind="Internal").ap()

    w_in_s = dscr("w_in_s", [L, D, 1536], BF16)
    w_out_s = dscr("w_out_s", [L, D, D], BF16)
    w_up_s = dscr("w_up_s", [L, D, DFF], BF16)
    w_down_s = dscr("w_down_s", [L, DFF, D], BF16)
    tabs_s = dscr("tabs_s", [L, 17, 128, 6 * 64], F32)

    dbg_outs = []

    with contextlib.ExitStack() as st:
        CAP = 212800
        sbt = st.enter_context(nc.sbuf_tensor("sb_all", [128, CAP], U8))
        ps = st.enter_context(nc.psum_tensor("ps_all", [128, 8, 512], F32))
        A = Arena(sbt, CAP)
        S = Sched(nc)

        pool_hist = []
        POOL_MAX_INFLIGHT = 4

        def do(eng, method, *args, R=(), W=(), K=None, **kw):
            if eng == "pool" and K is not None and len(pool_hist) >= POOL_MAX_INFLIGHT:
                tk = pool_hist[-POOL_MAX_INFLIGHT]
                if tk[2] > S.waited["pool"].get(('d', tk[1]), -1):
                    S.ops["pool"].append(dict(fn=None, deps={('d', tk[1]): tk[2]}, dma_key=None, signal=False))
                    S.waited["pool"][('d', tk[1])] = tk[2]
            tok = S.op(eng, lambda e: getattr(e, method)(*args, **kw), reads=list(R), writes=list(W), dma_key=K)
            if eng == "pool" and K is not None:
                pool_hist.append(tok)

        def bank(i):
            return ps[:, i, :]

        def bank_bf(i):
            return ps[:, i, :].bitcast(BF16).rearrange("p (a b) -> p a b", a=8)

        def BK(i):
            return ("bk", i)

        h_sb = A.alloc([NT, D], F32)
        hc_sb = A.alloc([NCT, D], F32)
        ident_bf = A.alloc([128], BF16)
        ident_f = A.alloc([128], F32)
        ones_f = A.alloc([128], F32)
        mask_bf = A.alloc([2, 512], BF16)
        cT_sb = A.alloc([8, 4], F32)
        sT = A.alloc([8, 4], F32)
        modT = A.alloc([L, 48, 4], F32)
        cols = A.alloc([L, 4, 8, 4], F32)
        gT_sb = A.alloc([L, 4, 8], F32)
        b_adaT_sb = A.alloc([L, 48], F32)
        qk_b = A.alloc([L, 4, 64], F32)
        sink_b = A.alloc([L, 8], F32)
        sinkexp = A.alloc([L, 8], F32)
        Grow = A.alloc([4, D], F32)
        rstd1 = A.alloc([NTT], F32)
        rstd2 = A.alloc([NTT], F32)
        sm = A.alloc([64], F32)
        tabslot = A.alloc([3, 6, 64], F32)
        persist_mark = A.mark()

        def dbg(name, ap, shape, dt, reads):
            if not DEV.get("dump"):
                return
            o = nc.dram_tensor("dbg_" + name, list(shape), dt, kind="ExternalOutput").ap()
            do("sp", "dma_start", out=o, in_=ap, R=reads, K="dbg")
            dbg_outs.append("dbg_" + name)

        def conv(dst, src, rows, key):
            for r0 in range(0, rows, 128):
                do("pool", "dma_start", out=dst[r0:r0 + 128, :], in_=src[r0:r0 + 128, :], max_dma_last_dim=8192,
                   W=[(key, r0)], K=key)

        wa_bf = [A.alloc([8, 768], BF16) for _ in range(4)]
        sT_bf = A.alloc([8, 4], BF16)

        def load_wada(l, slab):
            if True:
                do("pool", "dma_start", out=wa_bf[slab % 4],
                   in_=w_ada_d[l][:, slab * 768:(slab + 1) * 768].rearrange("(c p) f -> p c f", p=128),
                   max_dma_last_dim=8192, W=[("wa", slab % 4)], K=("wa", slab % 4))

        def cv_res(name, l, rows):
            return [((name, l), r0) for r0 in range(0, rows, 128)]

        tab_sb = A.alloc([17, 6, 64], F32)
        rope_gen = A.alloc([17, 2, 64], F32)
        masks_f = A.alloc([2, 512], F32)
        tmp84 = A.alloc([8, 4], F32)

        def ld(dst, src, res):
            do("sp", "dma_start", out=dst, in_=src, W=[res], K="c")

        ld(ident_f, ident_d, "ident_f")
        ld(cT_sb, cT_d, "cT_sb")
        ld(b_adaT_sb, b_adaT_d, "b_adaT_sb")
        ld(gT_sb, gT_d, "gT_sb")
        ld(rope_gen, rope_d, "rope_gen")
        ld(masks_f, masks_d, "masks_f")
        ld(qk_b.rearrange("p a b c -> p (a b c)"), qk_d.partition_broadcast(128), "qk_b")
        ld(sink_b.rearrange("p a b -> p (a b)"), sink_d.partition_broadcast(128), "sink_b")

        do("dve", "tensor_copy", out=ident_bf, in_=ident_f, R=["ident_f"], W=["ident_bf"])
        do("dve", "tensor_copy", out=mask_bf, in_=masks_f, R=["masks_f"], W=["mask_bf"])
        do("dve", "memset", ones_f, 1.0, W=["ones_f"])
        do("act", "activation", out=sT, in_=cT_sb, func=AF.Silu, R=["cT_sb"], W=["sT"])
        do("dve", "tensor_copy", out=sT_bf, in_=sT, R=["sT"], W=["sT_bf"])
        do("act", "activation", out=sinkexp, in_=sink_b, func=AF.Exp, R=["sink_b"], W=["sinkexp"])

        modps = bank(0)[:, 0:192].rearrange("p (a b) -> p a b", b=4)
        def mod_layer(l):
            for slab in range(4):
                load_wada(l, slab)
            if l == 0:
                conv(w_in_s[0], w_in_d[0], D, ("cv_in", 0))
                conv(w_out_s[0], w_out_d[0], D, ("cv_out", 0))
            for slab in range(8):
                wb = wa_bf[slab % 4]
                for fl in range(6):
                    ch = slab * 6 + fl
                    for c in range(8):
                        do("pe", "matmul", modps[:, ch, :], lhsT=wb[:, c, fl * 128:(fl + 1) * 128], rhs=sT_bf[:, c, :],
                           start=(c == 0), stop=(c == 7), R=[("wa", slab % 4), "sT_bf"], W=[BK(0)])
                if slab + 4 < 8:
                    load_wada(l, slab + 4)
            do("dve", "tensor_tensor", out=modT[:, l], in0=modps,
               in1=b_adaT_sb[:, l, :].unsqueeze(2).to_broadcast([128, 48, 4]), op=ALU.add,
               R=["b_adaT_sb"], W=[BK(0), ("modT", l)])
            for (kind, seg, gi, plus1) in ((0, 1, 0, True), (1, 2, 2, False), (2, 4, 1, True), (3, 5, 3, False)):
                src = modT[:, l, seg * 8:(seg + 1) * 8, :]
                gb = gT_sb[:, l, gi, :].unsqueeze(2).to_broadcast([128, 8, 4])
                if plus1:
                    do("dve", "tensor_scalar", out=tmp84, in0=src, scalar1=1.0, scalar2=None, op0=ALU.add,
                       R=[("modT", l)], W=["tmp84"])
                    do("dve", "tensor_tensor", out=cols[:, l, kind], in0=tmp84, in1=gb, op=ALU.mult,
                       R=["tmp84", "gT_sb"], W=[("cols", l)])
                else:
                    do("dve", "tensor_tensor", out=cols[:, l, kind], in0=src, in1=gb, op=ALU.mult,
                       R=[("modT", l), "gT_sb"], W=[("cols", l)])
            for v, (gsel, wsel) in enumerate(((0, 0), (1, 1), (0, 2), (1, 3))):
                do("dve", "tensor_tensor", out=tab_sb[:, :, v, :], in0=rope_gen[:, :, gsel, :],
                   in1=qk_b[:, l, wsel, :].unsqueeze(1).to_broadcast([128, 17, 64]), op=ALU.mult,
                   R=["rope_gen", "qk_b"], W=["tab_sb"])
            do("dve", "tensor_copy", out=tab_sb[:, :, 4:6, :], in_=rope_gen, R=["rope_gen"], W=["tab_sb"])
            do("sp", "dma_start", out=tabs_s[l].rearrange("t p f -> p t f"),
               in_=tab_sb.rearrange("p t v d -> p t (v d)"), R=["tab_sb"], W=[("tabs", l)], K=("tabs", l))

        mod_layer(0)
        for l in range(1, L):
            mod_layer(l)
        def late_convs():
            conv(w_up_s[0], w_up_d[0], D, ("cv_up", 0))
            conv(w_down_s[0], w_down_d[0], DFF, ("cv_down", 0))
            for l in range(1, L):
                conv(w_in_s[l], w_in_d[l], D, ("cv_in", l))
                conv(w_out_s[l], w_out_d[l], D, ("cv_out", l))
                conv(w_up_s[l], w_up_d[l], D, ("cv_up", l))
                conv(w_down_s[l], w_down_d[l], DFF, ("cv_down", l))

        if DEV.get("dump"):
            dbg("modT", modT, [128, L, 48, 4], F32, [("modT", l) for l in range(L)])
            dbg("cols", cols, [128, L, 4, 8, 4], F32, [("cols", l) for l in range(L)])
            dbg("sT", sT, [128, 8, 4], F32, ["sT"])
        A.reset(persist_mark)
        S.barrier(skip=lambda k: isinstance(k, tuple) and str(k[0]).startswith('cv_'))

        tab_ctr = [0]

        def load_tab(l, tsel):
            slot = tab_ctr[0] % 3
            tab_ctr[0] += 1
            do("sp", "dma_start", out=tabslot[:, slot].rearrange("p v d -> p (v d)"), in_=tabs_s[l, tsel],
               R=[("tabs", l)], W=[("tabslot", slot)], K=("tabslot", slot))
            return slot

        def rms_rstd(src_ap, n, ss_ap, out_ap, junk, res_src, res_out, banks=()):
            do("act", "activation", out=junk, in_=src_ap, func=AF.Square, accum_out=ss_ap,
               R=res_src, W=["junk", (res_out, "ss")] + [BK(x) for x in banks])
            do("act", "activation", out=ss_ap, in_=ss_ap, func=AF.Ln, scale=1.0 / n, bias=EPS,
               R=[(res_out, "ss")], W=[(res_out, "ss")])
            do("act", "activation", out=out_ap, in_=ss_ap, func=AF.Exp, scale=-0.5, R=[(res_out, "ss")], W=[res_out])

        for b in range(BPC):
            do("sp", "dma_start", out=h_sb, in_=x_in[b].rearrange("(t p) d -> p t d", p=128),
               W=[("h", t) for t in range(NT)], K="hload")
            do("sp", "dma_start", out=hc_sb, in_=ctx_in[b].rearrange("(t p) d -> p t d", p=128),
               W=[("hc", t) for t in range(NCT)], K="hload")

            for l in range(L):
                last = is_last[l]
                m0 = A.mark()

                def hres(tt):
                    return ("hc", tt) if tt < NCT else ("h", tt - NCT)

                def hsrc(tt):
                    return hc_sb[:, tt, :] if tt < NCT else h_sb[:, tt - NCT, :]

                def rsel(tt, b=b):
                    return 2 if tt < NCT else b

                def colap(kind, c, r, l=l):
                    return cols[:, l, kind, c, r:r + 1]

                def shap(seg, c, r, l=l):
                    return modT[:, l, seg * 8 + c, r:r + 1]

                diagb4 = [A.alloc([128], F32) for _ in range(4)]
                for gi, (kind, r) in enumerate(((1, b), (3, b), (1, 2), (3, 2))):
                    if r == 2 and last:
                        continue
                    for half in range(2):
                        for c4 in range(4):
                            c = half * 4 + c4
                            do("dve", "tensor_scalar", out=diagb4[c4], in0=ident_f, scalar1=colap(kind, c, r), scalar2=None,
                               op0=ALU.mult, R=["ident_f", ("cols", l)], W=[("diagb", c4)])
                            do("pe", "matmul", bank(gi % 2)[:, c4 * 128:(c4 + 1) * 128], lhsT=ones_f, rhs=diagb4[c4],
                               start=True, stop=True, R=["ones_f", ("diagb", c4)], W=[BK(gi % 2)])
                        do("act", "activation", out=Grow[:, gi, half * 512:(half + 1) * 512], in_=bank(gi % 2), func=AF.Copy,
                           W=[BK(gi % 2), ("Grow", gi)])

                kT_sb = A.alloc([2, NTT * 128], BF16)
                V_sb = A.alloc([NTT, 4, 66], BF16)
                w_kv = A.alloc([8, 512], BF16)
                w_q = A.alloc([8, 1024], BF16)
                w_o = A.alloc([8, 1024], BF16)
                hn2 = [A.alloc([D], BF16) for _ in range(3)]
                junk = A.alloc([D], BF16)
                uT2 = [A.alloc([8, 128], BF16) for _ in range(3)]
                sq_f = A.alloc([512], F32)
                t1 = A.alloc([16, 64], F32)
                t2 = A.alloc([16, 64], F32)
                q_tok = A.alloc([8, 2, 64], BF16)
                k_tok2 = [A.alloc([2, 2, 64], BF16) for _ in range(3)]
                qT2 = [A.alloc([16, 128], BF16) for _ in range(2)]
                NPT = 4
                PT = [A.alloc([512], BF16) for _ in range(NPT)]
                O_tok = A.alloc([16, 64], BF16)
                OT_sb = A.alloc([8, 128], BF16)
                tmpf = t1.rearrange("p h d -> p (h d)")
                ssm = A.alloc([8], F32)
                ss1 = A.alloc([NTT], F32)
                ssk2 = A.alloc([3, 2], F32)
                rstd_k2 = A.alloc([3, 2], F32)
                ssq = A.alloc([8], F32)
                rstd_q = A.alloc([8], F32)
                den = A.alloc([4], F32)
                rden = A.alloc([4], F32)

                wsrc = w_in_s[l].rearrange("(c p) f -> p c f", p=128)
                for i, c0 in enumerate((512, 1280, 640, 1408)):
                    do("sp", "dma_start", out=w_kv[:, :, i * 128:(i + 1) * 128], in_=wsrc[:, :, c0:c0 + 128],
                       R=cv_res("cv_in", l, D), W=["w_kv"], K="w_kv")
                for i, c0 in enumerate((0, 768)):
                    do("sp", "dma_start", out=w_q[:, :, i * 512:(i + 1) * 512], in_=wsrc[:, :, c0:c0 + 512],
                       R=cv_res("cv_in", l, D), W=["w_q"], K="w_q")
                do("sp", "dma_start", out=w_o, in_=w_out_s[l].rearrange("(c p) f -> p c f", p=128),
                   R=cv_res("cv_out", l, D), W=["w_o"], K="w_o")
                if b == 0 and l == 0:
                    S.op("pool", None, reads=["w_kv", "w_q", "w_o"] + [("h", t) for t in range(NT)] + [("hc", t) for t in range(NCT)])
                    late_convs()
                do("dve", "memset", V_sb, 1.0, W=["V_sb"])
                do("dve", "memset", qT2[0], 0.0, W=[("qT", 0)])
                do("dve", "memset", qT2[1], 0.0, W=[("qT", 1)])

                def front_a(tt, rstd_ap, rstd_res, have_rstd, tb, hn_ap, hn_res, junk_ap, ss_ap, hn_on_act=False):
                    if not have_rstd:
                        rms_rstd(hsrc(tt), D, ss_ap, rstd_ap, junk_ap, [hres(tt)], rstd_res)
                    if hn_on_act:
                        do("act", "activation", out=hn_ap, in_=hsrc(tt), func=AF.Copy, scale=rstd_ap,
                           R=[hres(tt), rstd_res], W=[hn_res])
                    else:
                        do("dve", "tensor_scalar", out=hn_ap, in0=hsrc(tt), scalar1=rstd_ap, scalar2=None, op0=ALU.mult,
                           R=[hres(tt), rstd_res], W=[hn_res])
                    tbv = bank_bf(tb)
                    for c in range(8):
                        do("pe", "transpose", out=tbv[:, c, :], in_=hn_ap[:, c * 128:(c + 1) * 128], identity=ident_bf,
                           R=[hn_res, "ident_bf"], W=[BK(tb)])

                def front_b(tt, acol_kind, sh_seg, tb, dst, dst_res):
                    r = rsel(tt)
                    tbv = bank_bf(tb)
                    for c in range(8):
                        do("dve", "tensor_scalar", out=dst[:, c, :], in0=tbv[:, c, :], scalar1=colap(acol_kind, c, r),
                           scalar2=shap(sh_seg, c, r), op0=ALU.mult, op1=ALU.add,
                           R=[("cols", l), ("modT", l)], W=[BK(tb), dst_res])

                def rope(src_ps, nh, slot, vC, vS, out_ap, rstd_bc, rstd_res, banks, out_res):
                    C = tabslot[:, slot, vC, :]
                    Sg = tabslot[:, slot, vS, :].rearrange("p (a b d) -> p a b d", a=2, b=2)
                    bw = [BK(x) for x in banks]
                    t1v = t1[:, 0:nh, :]
                    t2f = t2[:, 0:nh, :]
                    t2v = t2f.rearrange("p h (a b d) -> p h a b d", a=2, b=2)
                    s5 = src_ps.rearrange("p h (a b d) -> p h a b d", a=2, b=2)
                    do("dve", "tensor_tensor", out=t1v, in0=src_ps, in1=C.unsqueeze(1).to_broadcast([128, nh, 64]), op=ALU.mult,
                       R=[("tabslot", slot)], W=bw + ["t1"])
                    do("dve", "tensor_tensor", out=t2v[:, :, :, 0, :], in0=s5[:, :, :, 1, :],
                       in1=Sg[:, :, 0, :].unsqueeze(1).to_broadcast([128, nh, 2, 16]), op=ALU.mult,
                       R=[("tabslot", slot)], W=bw + ["t2"])
                    do("dve", "tensor_tensor", out=t2v[:, :, :, 1, :], in0=s5[:, :, :, 0, :],
                       in1=Sg[:, :, 1, :].unsqueeze(1).to_broadcast([128, nh, 2, 16]), op=ALU.mult,
                       R=[("tabslot", slot)], W=bw + ["t2"])
                    if rstd_bc is None:
                        do("dve", "tensor_tensor", out=out_ap, in0=t1v, in1=t2f, op=ALU.add, R=["t1", "t2"], W=[out_res])
                    else:
                        do("dve", "tensor_tensor", out=t1v, in0=t1v, in1=t2f, op=ALU.add, R=["t1", "t2"], W=["t1"])
                        do("dve", "tensor_tensor", out=out_ap, in0=t1v, in1=rstd_bc, op=ALU.mult, R=["t1", rstd_res], W=[out_res])

                def head_rstd_a(src_ps, nh, ss_ap, ss_res, banks):
                    bw = [BK(x) for x in banks]
                    do("act", "activation", out=sq_f[:, 0:nh * 64], in_=src_ps, func=AF.Square, W=bw + ["sq_f"])
                    do("dve", "tensor_reduce", out=ss_ap, in_=sq_f[:, 0:nh * 64].rearrange("p (h d) -> p h d", d=64),
                       axis=AX.X, op=ALU.add, R=["sq_f"], W=[ss_res])

                def head_rstd_b(ss_ap, ss_res, out_ap, out_res):
                    do("act", "activation", out=ss_ap, in_=ss_ap, func=AF.Ln, scale=1.0 / 64, bias=EPS, R=[ss_res], W=[ss_res])
                    do("act", "activation", out=out_ap, in_=ss_ap, func=AF.Exp, scale=-0.5, R=[ss_res], W=[out_res])

                slots1 = {}
                TB3 = [0, 1, 6]
                KV3 = [2, 3, 7]
                sqk = [A.alloc([128], F32) for _ in range(3)]

                def p1R(tt):
                    rms_rstd(hsrc(tt), D, ss1[:, tt:tt + 1], rstd1[:, tt:tt + 1], junk, [hres(tt)], ("rstd1", tt))

                def p1A1(tt):
                    pb = tt % 3
                    do("act", "activation", out=hn2[pb], in_=hsrc(tt), func=AF.Copy, scale=rstd1[:, tt:tt + 1],
                       R=[hres(tt), ("rstd1", tt)], W=[("hn", pb)])

                def p1A2(tt):
                    pb = tt % 3
                    tbv = bank_bf(TB3[pb])
                    for c in range(8):
                        do("pe", "transpose", out=tbv[:, c, :], in_=hn2[pb][:, c * 128:(c + 1) * 128], identity=ident_bf,
                           R=[("hn", pb), "ident_bf"], W=[BK(TB3[pb])])

                def p1A3(tt):
                    pb = tt % 3
                    front_b(tt, 0, 0, TB3[pb], uT2[pb], ("uT", pb))

                def p1B1a(tt):
                    pb = tt % 3
                    kvb = KV3[pb]
                    slots1[tt] = load_tab(l, 16 if tt < NCT else tt - NCT)
                    uT = uT2[pb]
                    for c in range(8):
                        do("pe", "matmul", bank(kvb), lhsT=uT[:, c, :], rhs=w_kv[:, c, :], start=(c == 0), stop=(c == 7),
                           R=[("uT", pb), "w_kv"], W=[BK(kvb)])

                def p1B1b(tt):
                    pb = tt % 3
                    kvb = KV3[pb]
                    kv = bank(kvb)
                    do("act", "activation", out=sqk[pb], in_=kv[:, 0:128], func=AF.Square, W=[BK(kvb), ("sqk", pb)])
                    do("act", "activation", out=V_sb[:, tt, :, 0:64], in_=kv[:, 256:512].rearrange("p (h d) -> p h d", d=64),
                       func=AF.Copy, W=[BK(kvb), "V_sb"])
                    do("dve", "tensor_reduce", out=ssk2[:, pb], in_=sqk[pb].rearrange("p (h d) -> p h d", d=64),
                       axis=AX.X, op=ALU.add, R=[("sqk", pb)], W=[("ssk", pb)])

                def p1B2(tt):
                    pb = tt % 3
                    kvb = KV3[pb]
                    slot = slots1[tt]
                    k_tok = k_tok2[pb]
                    kv = bank(kvb)
                    head_rstd_b(ssk2[:, pb], ("ssk", pb), rstd_k2[:, pb], ("rstd_k", pb))
                    rope(kv[:, 128:256].rearrange("p (h d) -> p h d", d=64), 2, slot, 4, 5, k_tok[:, :, 1, :], None, None, [kvb], ("k_tok", pb))
                    rope(kv[:, 0:128].rearrange("p (h d) -> p h d", d=64), 2, slot, 2, 3, k_tok[:, :, 0, :],
                         rstd_k2[:, pb].unsqueeze(2).to_broadcast([128, 2, 64]), ("rstd_k", pb), [kvb], ("k_tok", pb))

                def p1B3a(tt):
                    pb = tt % 3
                    ktbank = 4 + tt % 2
                    ktb = bank_bf(ktbank)
                    for j in range(2):
                        do("pe", "transpose", out=ktb[:, j, :], in_=k_tok2[pb][:, j].rearrange("p a d -> p (a d)"), identity=ident_bf,
                           R=[("k_tok", pb), "ident_bf"], W=[BK(ktbank)])

                def p1B3b(tt):
                    ktbank = 4 + tt % 2
                    ktb = bank_bf(ktbank)
                    do("act", "activation", out=kT_sb[:, :, tt * 128:(tt + 1) * 128], in_=ktb[:, 0:2, :], func=AF.Copy,
                       W=[BK(ktbank), "kT_sb"])

                stages = [(7, p1R), (5, p1A1), (4, p1A2), (3, p1A3), (2, p1B1a), (1, p1B1b), (0, p1B2), (-1, p1B3a), (-2, p1B3b)]
                for it in range(-7, NTT + 2):
                    for off, fn_ in stages:
                        if 0 <= it + off < NTT:
                            fn_(it + off)

                if DEV.get("dump") and b == 0 and l == 0:
                    dbg("rstd1", rstd1, [128, NTT], F32, [("rstd1", t) for t in range(NTT)])
                    dbg("kT", kT_sb, [128, 2, NTT * 128], BF16, ["kT_sb"])
                    dbg("V", V_sb, [128, NTT, 4, 66], BF16, ["V_sb"])

                q_tiles = list(range(NCT, NTT)) + ([] if last else list(range(NCT)))
                if DEV.get("max_q_tiles") is not None:
                    q_tiles = q_tiles[:DEV["max_q_tiles"]]
                SBANKS = [3, 4, 5]
                TBANK = 0
                tabs_of = {}
                uTq = uT2[0]

                def prepA(tt):
                    tabs_of[tt] = load_tab(l, 16 if tt < NCT else tt - NCT)
                    front_a(tt, rstd1[:, tt:tt + 1], ("rstd1", tt), True, TBANK, hn2[0], ("hn", 0), junk, None)
                    front_b(tt, 0, 0, TBANK, uTq, ("uT", 0))

                def prepB(tt):
                    for n in range(2):
                        for c in range(8):
                            do("pe", "matmul", bank(1 + n), lhsT=uTq[:, c, :], rhs=w_q[:, c, n * 512:(n + 1) * 512],
                               start=(c == 0), stop=(c == 7), R=[("uT", 0), "w_q"], W=[BK(1 + n)])

                def prepC(tt):
                    head_rstd_a(bank(1), 8, ssq, "ssq", [1])

                def prepD(tt):
                    slot = tabs_of[tt]
                    head_rstd_b(ssq, "ssq", rstd_q, "rstd_q")
                    rope(bank(1).rearrange("p (h d) -> p h d", d=64), 8, slot, 0, 1, q_tok[:, :, 0, :],
                         rstd_q.unsqueeze(2).to_broadcast([128, 8, 64]), "rstd_q", [1], "q_tok")
                    rope(bank(2).rearrange("p (h d) -> p h d", d=64), 8, slot, 4, 5, q_tok[:, :, 1, :], None, None, [2], "q_tok")

                def prepE(tt):
                    qtb = bank_bf(TBANK)
                    for j in range(8):
                        do("pe", "transpose", out=qtb[:, j, :], in_=q_tok[:, j].rearrange("p a d -> p (a d)"), identity=ident_bf,
                           R=["q_tok", "ident_bf"], W=[BK(TBANK)])

                def prepF(tt, qb):
                    qtb = bank_bf(TBANK)
                    do("dve", "tensor_copy", out=qT2[qb][0:64, 0:8, :], in_=qtb[0:64, :, :], W=[BK(TBANK), ("qT", qb)])
                    do("dve", "tensor_copy", out=qT2[qb][64:128, 8:16, :], in_=qtb[64:128, :, :], W=[BK(TBANK), ("qT", qb)])

                def tailA(tt):
                    otb = bank_bf(TBANK)
                    Of = O_tok.rearrange("p h d -> p (h d)")
                    for c in range(8):
                        do("pe", "transpose", out=otb[:, c, :], in_=Of[:, c * 128:(c + 1) * 128], identity=ident_bf,
                           R=["O_tok", "ident_bf"], W=[BK(TBANK)])
                    do("dve", "tensor_copy", out=OT_sb, in_=otb, W=[BK(TBANK), "OT_sb"])

                def tailB(tt):
                    for n in range(2):
                        for c in range(8):
                            do("pe", "matmul", bank(1 + n), lhsT=OT_sb[:, c, :], rhs=w_o[:, c, n * 512:(n + 1) * 512],
                               start=(c == 0), stop=(c == 7), R=["OT_sb", "w_o"], W=[BK(1 + n)])

                def tailC(tt):
                    mix = ps[:, 1:3, :].rearrange("p a b -> p (a b)")
                    rms_rstd(mix, D, ssm[:, 0:1], ssm[:, 1:2], junk, [], "rstd_m", banks=(1, 2))
                    gsel = 2 if tt < NCT else 0
                    do("dve", "scalar_tensor_tensor", out=tmpf, in0=mix, scalar=ssm[:, 1:2], in1=Grow[:, gsel, :],
                       op0=ALU.mult, op1=ALU.mult, R=["rstd_m", ("Grow", gsel)], W=[BK(1), BK(2), "tmpf"])
                    do("dve", "tensor_tensor", out=hsrc(tt), in0=hsrc(tt), in1=tmpf, op=ALU.add,
                       R=["tmpf", hres(tt)], W=[hres(tt)])

                def make_steps(tt):
                    is_ctx = tt < NCT
                    i_lat = tt - NCT
                    steps = []
                    for g in range(4):
                        ab, kvh = g // 2, g % 2
                        if is_ctx:
                            kbs = [(0, None), (1, None)]
                        elif ab == 0:
                            kbs = [(kb, None) for kb in range(NTT)]
                        else:
                            kbs = []
                            if i_lat > 0:
                                kbs.append((NCT + i_lat - 1, 0))
                            kbs.append((NCT + i_lat, None))
                            if i_lat < NT - 1:
                                kbs.append((NCT + i_lat + 1, 1))
                            kbs += [(0, None), (1, None)]
                        for ki, (kb, m) in enumerate(kbs):
                            steps.append(dict(g=g, ab=ab, kvh=kvh, kb=kb, m=m, first=(ki == 0), lastk=(ki == len(kbs) - 1)))
                    return steps

                sctr = [0]

                def attn(tt, qb, hooks):
                    steps = make_steps(tt)
                    ns = len(steps)
                    base = sctr[0]
                    sctr[0] += ns
                    qT_sb = qT2[qb]

                    def emit_S(si):
                        s = steps[si]
                        sb_ = SBANKS[(base + si) % 3]
                        h0 = s['ab'] * 8 + s['kvh'] * 4
                        rhs = qT_sb[:, h0:h0 + 4, :].rearrange("p h q -> p (h q)")
                        do("pe", "matmul", bank(sb_), lhsT=kT_sb[:, s['kvh'], s['kb'] * 128:(s['kb'] + 1) * 128], rhs=rhs,
                           start=True, stop=(s['m'] is None), R=["kT_sb", ("qT", qb)], W=[BK(sb_)])
                        if s['m'] is not None:
                            do("pe", "matmul", bank(sb_), lhsT=ident_bf, rhs=mask_bf[:, s['m'], :], start=False, stop=True,
                               R=["ident_bf", "mask_bf"], W=[BK(sb_)])

                    def emit_EXP_PV(si):
                        s = steps[si]
                        sb_ = SBANKS[(base + si) % 3]
                        pslot = (base + si) % NPT
                        ob = 6 + s['g'] % 2
                        obv = bank(ob).rearrange("p (h x) -> p h x", h=4)
                        do("act", "activation", out=PT[pslot], in_=bank(sb_), func=AF.Exp, scale=0.125,
                           W=[BK(sb_), ("PT", pslot)])
                        for hh in range(4):
                            do("pe", "matmul", obv[:, hh, 0:65], lhsT=PT[pslot][:, hh * 128:(hh + 1) * 128],
                               rhs=V_sb[:, s['kb'], s['ab'] * 2 + s['kvh'], 0:65],
                               start=(s['first'] and hh == 0), stop=s['lastk'], skip_group_check=True,
                               R=[("PT", pslot), "V_sb"], W=[BK(ob)])
                        if s['lastk']:
                            g = s['g']
                            if s['ab'] == 1:
                                do("dve", "tensor_tensor", out=den, in0=obv[:, :, 64], in1=sinkexp[:, l, s['kvh'] * 4:(s['kvh'] + 1) * 4],
                                   op=ALU.add, R=["sinkexp"], W=[BK(ob), "den"])
                                do("dve", "reciprocal", out=rden, in_=den, R=["den"], W=["rden"])
                            else:
                                do("dve", "reciprocal", out=rden, in_=obv[:, :, 64], W=[BK(ob), "rden"])
                            do("dve", "tensor_tensor", out=O_tok[:, g * 4:(g + 1) * 4, :], in0=obv[:, :, 0:64],
                               in1=rden.unsqueeze(2).to_broadcast([128, 4, 64]), op=ALU.mult,
                               R=["rden"], W=[BK(ob), "O_tok"])

                    for si in range(min(3, ns)):
                        emit_S(si)
                    for si in range(ns):
                        emit_EXP_PV(si)
                        if si + 3 < ns:
                            emit_S(si + 3)
                        for f in hooks.get(si, ()):
                            f()

                HK = DEV.get("hooks", dict(tA=0, tB=2, tC=5, pA=7, pB=12, pC=18, pD=20, pE=34, pF=37))
                if q_tiles:
                    t0_ = q_tiles[0]
                    prepA(t0_); prepB(t0_); prepC(t0_); prepD(t0_); prepE(t0_); prepF(t0_, 0)
                for idx, tt in enumerate(q_tiles):
                    ns_ = len(make_steps(tt))
                    hooks = {}

                    def at(step, f, hooks=hooks, ns_=ns_):
                        hooks.setdefault(min(step, ns_ - 1), []).append(f)

                    if idx > 0:
                        prev = q_tiles[idx - 1]
                        at(HK['tA'], lambda prev=prev: tailA(prev))
                        at(HK['tB'], lambda prev=prev: tailB(prev))
                        at(HK['tC'], lambda prev=prev: tailC(prev))
                    if idx + 1 < len(q_tiles):
                        nxt = q_tiles[idx + 1]
                        nqb = (idx + 1) % 2
                        at(HK['pA'], lambda nxt=nxt: prepA(nxt))
                        at(HK['pB'], lambda nxt=nxt: prepB(nxt))
                        at(HK['pC'], lambda nxt=nxt: prepC(nxt))
                        at(HK['pD'], lambda nxt=nxt: prepD(nxt))
                        at(HK['pE'], lambda nxt=nxt: prepE(nxt))
                        at(HK['pF'], lambda nxt=nxt, nqb=nqb: prepF(nxt, nqb))
                    attn(tt, idx % 2, hooks)
                if q_tiles:
                    tailA(q_tiles[-1]); tailB(q_tiles[-1]); tailC(q_tiles[-1])

                if DEV.get("dump") and b == 0 and l == 0:
                    dbg("h_p2", h_sb, [128, NT, D], F32, [("h", t) for t in range(NT)])
                    dbg("hc_p2", hc_sb, [128, NCT, D], F32, [("hc", t) for t in range(NCT)])

                A.reset(m0)
                S.barrier(skip=lambda k: isinstance(k, tuple) and str(k[0]).startswith('cv_'))
                if not DEV.get("skip_mlp"):
                    mlp_tiles = list(range(NCT, NTT)) + ([] if last else list(range(NCT)))
                    ntl = len(mlp_tiles)
                    u2T = A.alloc([8, NTT * 128], BF16)
                    wu_buf = [A.alloc([8, 512], BF16) for _ in range(3)]
                    wd_buf = [A.alloc([4, D], BF16) for _ in range(3)]
                    hr_f = [A.alloc([384], F32) for _ in range(2)]
                    hT = [A.alloc([384], BF16) for _ in range(2)]
                    hn3 = A.alloc([D], BF16)
                    junk3 = A.alloc([D], BF16)
                    tmp3 = A.alloc([D], F32)
                    ssy = A.alloc([8], F32)
                    hn3 = [hn3, A.alloc([D], BF16)]
                    ss3 = A.alloc([NTT], F32)
                    for ti, tt in enumerate(mlp_tiles):
                        front_a(tt, rstd2[:, tt:tt + 1], ("rstd2", tt), False, ti % 2, hn3[ti % 2], ("hn", ti % 2), junk3, ss3[:, tt:tt + 1], hn_on_act=True)
                        front_b(tt, 2, 3, ti % 2, u2T[:, :, ti * 128:(ti + 1) * 128], ("u2T", ti))
                    if ntl == 18:
                        blocks = [list(range(i, i + 3)) for i in range(0, 18, 3)]
                    else:
                        blocks = [[0, 1, 2], [3, 4, 5], [6, 7, 8], [9, 10, 11], [12, 13], [14, 15]]
                    if DEV.get("max_blocks") is not None:
                        blocks = blocks[:DEV["max_blocks"]]
                    wctr = [0]
                    for blk in blocks:
                        nt_b = len(blk)
                        ntok = nt_b * 128
                        tok0 = blk[0] * 128
                        wslots = {}

                        def emit_U(f):
                            fg, fl = f // 4, f % 4
                            if fl == 0:
                                wslot = wctr[0] % 3
                                wctr[0] += 1
                                do("sp", "dma_start", out=wu_buf[wslot],
                                   in_=w_up_s[l][:, fg * 512:(fg + 1) * 512].rearrange("(c p) f -> p c f", p=128),
                                   R=cv_res("cv_up", l, D), W=[("wu", wslot)], K=("wu", wslot))
                                do("sp", "dma_start", out=wd_buf[wslot],
                                   in_=w_down_s[l][fg * 512:(fg + 1) * 512, :].rearrange("(fl p) n -> p fl n", p=128),
                                   R=cv_res("cv_down", l, DFF), W=[("wd", wslot)], K=("wd", wslot))
                                wslots[fg] = wslot
                            ws = wslots[fg]
                            ub = f % 2
                            for c in range(8):
                                do("pe", "matmul", bank(ub)[:, 0:ntok], lhsT=wu_buf[ws][:, c, fl * 128:(fl + 1) * 128],
                                   rhs=u2T[:, c, tok0:tok0 + ntok], start=(c == 0), stop=(c == 7),
                                   R=[("wu", ws)] + [("u2T", ti) for ti in blk], W=[BK(ub)])
                            do("act", "activation", out=hr_f[ub][:, 0:ntok], in_=bank(ub)[:, 0:ntok], func=AF.Relu,
                               W=[BK(ub), ("hr", ub)])
                            do("act", "activation", out=hT[ub][:, 0:ntok], in_=hr_f[ub][:, 0:ntok], func=AF.Square,
                               R=[("hr", ub)], W=[("hT", ub)])

                        def emit_D(f):
                            fg, fl = f // 4, f % 4
                            ws = wslots[fg]
                            ub = f % 2
                            for t in range(nt_b):
                                for n in range(2):
                                    yb = 2 + t * 2 + n
                                    do("pe", "matmul", bank(yb), lhsT=hT[ub][:, t * 128:(t + 1) * 128],
                                       rhs=wd_buf[ws][:, fl, n * 512:(n + 1) * 512], start=(f == 0), stop=(f == 31),
                                       R=[("hT", ub), ("wd", ws)], W=[BK(yb)])

                        emit_U(0)
                        for f in range(32):
                            if f + 1 < 32:
                                emit_U(f + 1)
                            emit_D(f)
                        for t, ti in enumerate(blk):
                            tt = mlp_tiles[ti]
                            y = ps[:, 2 + 2 * t:4 + 2 * t, :].rearrange("p a b -> p (a b)")
                            rms_rstd(y, D, ssy[:, 0:1], ssy[:, 1:2], junk3, [], "rstd_y", banks=(2 + 2 * t, 3 + 2 * t))
                            gsel = 3 if tt < NCT else 1
                            do("dve", "scalar_tensor_tensor", out=tmp3, in0=y, scalar=ssy[:, 1:2], in1=Grow[:, gsel, :],
                               op0=ALU.mult, op1=ALU.mult, R=["rstd_y", ("Grow", gsel)],
                               W=[BK(2 + 2 * t), BK(3 + 2 * t), "tmpf"])
                            do("dve", "tensor_tensor", out=hsrc(tt), in0=hsrc(tt), in1=tmp3, op=ALU.add,
                               R=["tmpf", hres(tt)], W=[hres(tt)])
                A.reset(m0)
                S.barrier(skip=lambda k: isinstance(k, tuple) and str(k[0]).startswith('cv_'))

            do("sp", "dma_start", out=h_out[b].rearrange("(t p) d -> p t d", p=128), in_=h_sb,
               R=[("h", t) for t in range(NT)], K="out")
            if emit_hc:
                do("sp", "dma_start", out=hc_out[b].rearrange("(t p) d -> p t d", p=128), in_=hc_sb,
                   R=[("hc", t) for t in range(NCT)], K="out")

        S.wait_dma_all("sp", "out")
        if dbg_outs:
            S.wait_dma_all("sp", "dbg")
        S.emit()
        build_program.stats = S.stats
    return nc, dbg_outs


def _rope_tables():
    theta = np.float32(10000.0)
    axis_dim = 32
    inv = (theta ** (-(np.arange(0, axis_dim, 2, dtype=np.float32) / np.float32(axis_dim)))).astype(np.float32)
    tok = np.arange(SEQ)
    row = (tok // 64).astype(np.float32)
    col = (tok % 64).astype(np.float32)
    ang_r = row[:, None] * inv[None, :]
    ang_c = col[:, None] * inv[None, :]
    cr, sr, cc, sc = np.cos(ang_r), np.sin(ang_r), np.cos(ang_c), np.sin(ang_c)
    C = np.concatenate([cr, cr, cc, cc], axis=1).astype(np.float32)
    Sg = np.concatenate([-sr, sr, -sc, sc], axis=1).astype(np.float32)
    out = np.zeros((128, 17, 2, 64), np.float32)
    out[:, :16, 0, :] = C.reshape(16, 128, 64).transpose(1, 0, 2)
    out[:, :16, 1, :] = Sg.reshape(16, 128, 64).transpose(1, 0, 2)
    out[:, 16, 0, :] = 1.0
    return out


def _masks():
    kk = np.arange(128)[:, None]
    r = np.arange(128)[None, :]
    mprev = np.where(kk >= r, 0.0, NEG).astype(np.float32)
    mnext = np.where(kk <= r, 0.0, NEG).astype(np.float32)
    m = np.zeros((128, 2, 512), np.float32)
    m[:, 0, :] = np.tile(mprev, (1, 4))
    m[:, 1, :] = np.tile(mnext, (1, 4))
    return m


def _swap16(v):
    v4 = v.reshape(v.shape[:-1] + (2, 2, 16))
    return np.ascontiguousarray(v4[..., ::-1, :]).reshape(v.shape)


def _common_inputs(ls, c, c_ctx, w_ada, b_ada, g_pre_mix, g_post_mix, g_pre_mlp, g_post_mlp,
                   w_in, q_norm, k_norm, sink, w_out, w_up, w_down):
    L = len(ls)
    f = lambda a: np.ascontiguousarray(np.asarray(a, dtype=np.float32))
    sl = slice(ls[0], ls[-1] + 1)
    b_adaT = f(np.asarray(b_ada)[sl].reshape(L, 48, 128).transpose(2, 0, 1))
    g4 = np.stack([np.asarray(g_pre_mix)[sl], np.asarray(g_pre_mlp)[sl], np.asarray(g_post_mix)[sl], np.asarray(g_post_mlp)[sl]], axis=1)
    gT = f(g4.reshape(L, 4, 8, 128).transpose(3, 0, 1, 2))
    qn = np.asarray(q_norm)[sl]
    kn = np.asarray(k_norm)[sl]
    qk = f(np.stack([qn, _swap16(qn), kn, _swap16(kn)], axis=1).reshape(-1))
    common = {
        "w_ada": f(np.asarray(w_ada)[sl]), "b_adaT": b_adaT, "gT": gT, "qk_norm": qk,
        "sink": f(np.asarray(sink)[sl].reshape(-1)),
        "w_in": f(np.asarray(w_in)[sl]), "w_out": f(np.asarray(w_out)[sl]),
        "w_up": f(np.asarray(w_up)[sl]), "w_down": f(np.asarray(w_down)[sl]),
        "ident": np.eye(128, dtype=np.float32), "rope_cs": _rope_tables(), "masks": _masks(),
    }
    cTs = []
    c = np.asarray(c, dtype=np.float32)
    c_ctx = np.asarray(c_ctx, dtype=np.float32)
    for core in range(NCORES):
        rows = np.zeros((4, D), np.float32)
        rows[0] = c[core * BPC]
        rows[1] = c[core * BPC + 1]
        rows[2] = c_ctx
        cTs.append(f(rows.reshape(4, 8, 128).transpose(2, 1, 0)))
    return common, cTs


_PROG_CACHE = {}


def _get_prog(L, is_last, emit_hc):
    key = (L, tuple(is_last), emit_hc, repr(sorted(DEV.items())))
    if key not in _PROG_CACHE:
        _PROG_CACHE[key] = build_program(L, list(is_last), emit_hc)
    return _PROG_CACHE[key]


def _launch(ls, is_last, emit_hc, h, hc, kw):
    nc, dbg_names = _get_prog(len(ls), is_last, emit_hc)
    common, cTs = _common_inputs(ls, **kw)
    in_maps = []
    for core in range(NCORES):
        m = dict(common)
        m["x_in"] = np.ascontiguousarray(h[core * BPC:(core + 1) * BPC])
        m["ctx_in"] = np.ascontiguousarray(hc[core * BPC:(core + 1) * BPC])
        m["cT"] = cTs[core]
        in_maps.append(m)
    res = run_bass_kernel_spmd(nc, in_maps, core_ids=list(range(NCORES)))
    h_new = np.concatenate([r["h_out"] for r in res.results], axis=0)
    hc_new = None
    if emit_hc:
        hc_new = np.concatenate([r["hc_out"] for r in res.results], axis=0)
    return h_new, hc_new, res


FUSED = True


def kernel(x, c, ctx, c_ctx, w_ada, b_ada, g_pre_mix, g_post_mix, g_pre_mlp, g_post_mlp,
           w_in, q_norm, k_norm, sink, w_out, w_up, w_down):
    kw = dict(c=c, c_ctx=c_ctx, w_ada=w_ada, b_ada=b_ada, g_pre_mix=g_pre_mix, g_post_mix=g_post_mix,
              g_pre_mlp=g_pre_mlp, g_post_mlp=g_post_mlp, w_in=w_in, q_norm=q_norm, k_norm=k_norm,
              sink=sink, w_out=w_out, w_up=w_up, w_down=w_down)
    h = np.asarray(x, dtype=np.float32)
    hc = np.asarray(ctx, dtype=np.float32)
    if FUSED:
        h, _, _ = _launch([0, 1], [False, True], False, h, hc, kw)
    else:
        h, hc, _ = _launch([0], [False], True, h, hc, kw)
        h, _, _ = _launch([1], [True], False, h, hc, kw)
    return h.astype(np.float32)
```

```python
import contextlib
import numpy as np
import concourse.bass as bass
import concourse.mybir as mybir
from concourse.bass_utils import run_bass_kernel_spmd

F32 = mybir.dt.float32
BF16 = mybir.dt.bfloat16
U8 = mybir.dt.uint8
ALU = mybir.AluOpType
AF = mybir.ActivationFunctionType
AX = mybir.AxisListType

D = 1024
SEQ = 2048
CTX = 256
NT = 16
NCT = 2
NTT = NT + NCT
DFF = 4096
EPS = 1e-6
NEG = -30000.0
NCORES = 8
BPC = 2

DEV = {}

ENGS = ["pe", "act", "dve", "pool", "sp"]


class Sched:
    def __init__(self, nc):
        self.nc = nc
        self.ops = {e: [] for e in ENGS}
        self.last_w = {}
        self.readers = {}
        self.dma_cnt = {}
        self.dma_keys = []
        self.waited = {e: {} for e in ENGS}

    def _add_dep(self, eng, deps, tok, same_ok):
        if tok is None:
            return
        if tok[0] == 'e' and tok[1] == eng and (not same_ok or eng == 'pe'):
            return
        src = (tok[0], tok[1])
        val = tok[2]
        if tok[0] == 'd':
            val = self.dma_cnt[tok[1]]
        if val <= self.waited[eng].get(src, -1):
            return
        if val > deps.get(src, -1):
            deps[src] = val

    def op(self, eng, fn, reads=(), writes=(), dma_key=None):
        deps = {}
        for r in reads:
            self._add_dep(eng, deps, self.last_w.get(r), True)
        for r in writes:
            self._add_dep(eng, deps, self.last_w.get(r), False)
            for t in self.readers.get(r, ()):
                self._add_dep(eng, deps, t, False)
        for src, v in deps.items():
            self.waited[eng][src] = v
        idx = len(self.ops[eng])
        if dma_key is not None:
            if dma_key not in self.dma_cnt:
                self.dma_cnt[dma_key] = 0
                self.dma_keys.append(dma_key)
            self.dma_cnt[dma_key] += 1
            tok = ('d', dma_key, self.dma_cnt[dma_key])
        else:
            tok = ('e', eng, idx)
        self.ops[eng].append(dict(fn=fn, deps=deps, dma_key=dma_key, signal=False))
        if fn is None:
            return None
        for r in reads:
            self.readers.setdefault(r, []).append(tok)
        for r in writes:
            self.last_w[r] = tok
            self.readers[r] = []
        return tok

    def barrier(self, skip=lambda k: False):
        last = {}
        for e in ENGS:
            for i in range(len(self.ops[e]) - 1, -1, -1):
                o = self.ops[e][i]
                if o['fn'] is not None and o['dma_key'] is None:
                    last[e] = i
                    break
        for e in ENGS:
            deps = {}
            for src, i in last.items():
                if src != e and i > self.waited[e].get(('e', src), -1):
                    deps[('e', src)] = i
            for k, c in self.dma_cnt.items():
                if skip(k):
                    continue
                if c > self.waited[e].get(('d', k), -1):
                    deps[('d', k)] = c
            for src, v in deps.items():
                self.waited[e][src] = v
            self.ops[e].append(dict(fn=None, deps=deps, dma_key=None, signal=False))

    def wait_dma_all(self, eng, key):
        self.ops[eng].append(dict(fn=None, deps={('d', key): self.dma_cnt[key]}, dma_key=None, signal=False))

    def emit(self):
        nc = self.nc
        for e in ENGS:
            for o in self.ops[e]:
                for (kind, src), v in o['deps'].items():
                    if kind == 'e':
                        self.ops[src][v]['signal'] = True
        rank = {e: {} for e in ENGS}
        for e in ENGS:
            c = 0
            for i, o in enumerate(self.ops[e]):
                if o['signal']:
                    assert o['dma_key'] is None and o['fn'] is not None
                    c += 1
                    rank[e][i] = c
        with contextlib.ExitStack() as st:
            esem = {e: st.enter_context(nc.semaphore("s_" + e)) for e in ENGS}
            dsem = {k: st.enter_context(nc.semaphore("d_%d" % i)) for i, k in enumerate(self.dma_keys)}
            block = st.enter_context(nc.Block())

            def body(e):
                def run(engine):
                    for i, o in enumerate(self.ops[e]):
                        for (kind, src), v in o['deps'].items():
                            if kind == 'e':
                                engine.wait_ge(esem[src], rank[src][v])
                            else:
                                engine.wait_ge(dsem[src], 16 * v)
                        if o['fn'] is None:
                            continue
                        ins = o['fn'](engine)
                        if o['dma_key'] is not None:
                            ins.then_inc(dsem[o['dma_key']], 16)
                        elif o['signal']:
                            ins.then_inc(esem[e], 1)
                return run

            block.tensor(body("pe"))
            block.scalar(body("act"))
            block.vector(body("dve"))
            block.gpsimd(body("pool"))
            block.sync(body("sp"))
        self.stats = {e: (len(self.ops[e]), len(rank[e])) for e in ENGS}


def _dtsize(dt):
    return {F32: 4, BF16: 2, U8: 1}[dt]


class Arena:
    def __init__(self, t, cap):
        self.t = t
        self.cap = cap
        self.off = 0

    def mark(self):
        return self.off

    def reset(self, m):
        self.off = m

    def alloc(self, free_shape, dt):
        n = int(np.prod(free_shape)) * _dtsize(dt)
        off = (self.off + 31) // 32 * 32
        assert off + n <= self.cap, ("SBUF arena overflow", off + n, self.cap)
        self.off = off + n
        ap = self.t[:, off:off + n].bitcast(dt)
        if len(free_shape) == 2:
            ap = ap.rearrange("p (a b) -> p a b", a=free_shape[0])
        elif len(free_shape) == 3:
            ap = ap.rearrange("p (a b c) -> p a b c", a=free_shape[0], b=free_shape[1])
        elif len(free_shape) == 4:
            ap = ap.rearrange("p (a b c d) -> p a b c d", a=free_shape[0], b=free_shape[1], c=free_shape[2])
        return ap


def build_program(L, is_last, emit_hc):
    nc = bass.Bass("TRN2", target_bir_lowering=False)

    def din(name, shape, dt=F32):
        return nc.dram_tensor(name, list(shape), dt, kind="ExternalInput").ap()

    x_in = din("x_in", [BPC, SEQ, D])
    ctx_in = din("ctx_in", [BPC, CTX, D])
    cT_d = din("cT", [128, 8, 4])
    w_ada_d = din("w_ada", [L, D, 6 * D])
    b_adaT_d = din("b_adaT", [128, L, 48])
    gT_d = din("gT", [128, L, 4, 8])
    qk_d = din("qk_norm", [L * 4 * 64])
    sink_d = din("sink", [L * 8])
    w_in_d = din("w_in", [L, D, 1536])
    w_out_d = din("w_out", [L, D, D])
    w_up_d = din("w_up", [L, D, DFF])
    w_down_d = din("w_down", [L, DFF, D])
    ident_d = din("ident", [128, 128])
    rope_d = din("rope_cs", [128, 17, 2, 64])
    masks_d = din("masks", [128, 2, 512])
    h_out = nc.dram_tensor("h_out", [BPC, SEQ, D], F32, kind="ExternalOutput").ap()
    hc_out = None
    if emit_hc:
        hc_out = nc.dram_tensor("hc_out", [BPC, CTX, D], F32, kind="ExternalOutput").ap()

    def dscr(name, shape, dt):
        return nc.dram_tensor(name, list(shape), dt, kind="Internal").ap()

    w_in_s = dscr("w_in_s", [L, D, 1536], BF16)
    w_out_s = dscr("w_out_s", [L, D, D], BF16)
    w_up_s = dscr("w_up_s", [L, D, DFF], BF16)
    w_down_s = dscr("w_down_s", [L, DFF, D], BF16)
    tabs_s = dscr("tabs_s", [L, 17, 128, 6 * 64], F32)

    dbg_outs = []

    with contextlib.ExitStack() as st:
        CAP = 212800
        sbt = st.enter_context(nc.sbuf_tensor("sb_all", [128, CAP], U8))
        ps = st.enter_context(nc.psum_tensor("ps_all", [128, 8, 512], F32))
        A = Arena(sbt, CAP)
        S = Sched(nc)

        pool_hist = []
        POOL_MAX_INFLIGHT = [4]

        def do(eng, method, *args, R=(), W=(), K=None, **kw):
            if eng == "pool" and K is not None and len(pool_hist) >= POOL_MAX_INFLIGHT[0]:
                tk = pool_hist[-POOL_MAX_INFLIGHT[0]]
                if tk[2] > S.waited["pool"].get(('d', tk[1]), -1):
                    S.ops["pool"].append(dict(fn=None, deps={('d', tk[1]): tk[2]}, dma_key=None, signal=False))
                    S.waited["pool"][('d', tk[1])] = tk[2]
            tok = S.op(eng, lambda e: getattr(e, method)(*args, **kw), reads=list(R), writes=list(W), dma_key=K)
            if eng == "pool" and K is not None:
                pool_hist.append(tok)

        def bank(i):
            return ps[:, i, :]

        def bank_bf(i):
            return ps[:, i, :].bitcast(BF16).rearrange("p (a b) -> p a b", a=8)

        def BK(i):
            return ("bk", i)

        h_sb = A.alloc([NT, D], F32)
        hc_sb = A.alloc([NCT, D], F32)
        ident_bf = A.alloc([128], BF16)
        ident_f = A.alloc([128], F32)
        ones_f = A.alloc([128], F32)
        mask_bf = A.alloc([2, 512], BF16)
        cT_sb = A.alloc([8, 4], F32)
        sT = A.alloc([8, 4], F32)
        modT = A.alloc([L, 48, 4], F32)
        cols = A.alloc([L, 4, 8, 4], F32)
        gT_sb = A.alloc([L, 4, 8], F32)
        b_adaT_sb = A.alloc([L, 48], F32)
        qk_b = A.alloc([L, 4, 64], F32)
        sink_b = A.alloc([L, 8], F32)
        sinkexp = A.alloc([L, 8], F32)
        Grow = A.alloc([4, D], F32)
        rstd1 = A.alloc([NTT], F32)
        rstd2 = A.alloc([NTT], F32)
        sm = A.alloc([64], F32)
        tabslot = A.alloc([3, 6, 64], F32)
        persist_mark = A.mark()

        def dbg(name, ap, shape, dt, reads):
            if not DEV.get("dump"):
                return
            o = nc.dram_tensor("dbg_" + name, list(shape), dt, kind="ExternalOutput").ap()
            do("sp", "dma_start", out=o, in_=ap, R=reads, K="dbg")
            dbg_outs.append("dbg_" + name)

        def conv(dst, src, rows, key):
            for r0 in range(0, rows, 128):
                do("pool", "dma_start", out=dst[r0:r0 + 128, :], in_=src[r0:r0 + 128, :], max_dma_last_dim=8192,
                   W=[(key, r0)], K=key)

        wa_bf = [A.alloc([8, 768], BF16) for _ in range(4)]
        sT_bf = A.alloc([8, 4], BF16)

        def load_wada(l, slab):
            if True:
                do("pool", "dma_start", out=wa_bf[slab % 4],
                   in_=w_ada_d[l][:, slab * 768:(slab + 1) * 768].rearrange("(c p) f -> p c f", p=128),
                   max_dma_last_dim=8192, W=[("wa", slab % 4)], K=("wa", slab % 4))

        def cv_res(name, l, rows):
            return [((name, l), r0) for r0 in range(0, rows, 128)]

        tab_sb = A.alloc([17, 6, 64], F32)
        rope_gen = A.alloc([17, 2, 64], F32)
        masks_f = A.alloc([2, 512], F32)
        tmp84 = A.alloc([8, 4], F32)

        def ld(dst, src, res):
            do("sp", "dma_start", out=dst, in_=src, W=[res], K="c")

        ld(ident_f, ident_d, "ident_f")
        ld(cT_sb, cT_d, "cT_sb")
        ld(b_adaT_sb, b_adaT_d, "b_adaT_sb")
        ld(gT_sb, gT_d, "gT_sb")
        ld(rope_gen, rope_d, "rope_gen")
        ld(masks_f, masks_d, "masks_f")
        ld(qk_b.rearrange("p a b c -> p (a b c)"), qk_d.partition_broadcast(128), "qk_b")
        ld(sink_b.rearrange("p a b -> p (a b)"), sink_d.partition_broadcast(128), "sink_b")

        do("dve", "tensor_copy", out=ident_bf, in_=ident_f, R=["ident_f"], W=["ident_bf"])
        do("dve", "tensor_copy", out=mask_bf, in_=masks_f, R=["masks_f"], W=["mask_bf"])
        do("dve", "memset", ones_f, 1.0, W=["ones_f"])
        do("act", "activation", out=sT, in_=cT_sb, func=AF.Silu, R=["cT_sb"], W=["sT"])
        do("dve", "tensor_copy", out=sT_bf, in_=sT, R=["sT"], W=["sT_bf"])
        do("act", "activation", out=sinkexp, in_=sink_b, func=AF.Exp, R=["sink_b"], W=["sinkexp"])

        modps = bank(0)[:, 0:192].rearrange("p (a b) -> p a b", b=4)
        def mod_layer(l):
            for slab in range(4):
                load_wada(l, slab)
            if l == 0:
                conv(w_in_s[0], w_in_d[0], D, ("cv_in", 0))
                conv(w_out_s[0], w_out_d[0], D, ("cv_out", 0))
            for slab in range(8):
                wb = wa_bf[slab % 4]
                for fl in range(6):
                    ch = slab * 6 + fl
                    for c in range(8):
                        do("pe", "matmul", modps[:, ch, :], lhsT=wb[:, c, fl * 128:(fl + 1) * 128], rhs=sT_bf[:, c, :],
                           start=(c == 0), stop=(c == 7), R=[("wa", slab % 4), "sT_bf"], W=[BK(0)])
                if slab + 4 < 8:
                    load_wada(l, slab + 4)
            do("dve", "tensor_tensor", out=modT[:, l], in0=modps,
               in1=b_adaT_sb[:, l, :].unsqueeze(2).to_broadcast([128, 48, 4]), op=ALU.add,
               R=["b_adaT_sb"], W=[BK(0), ("modT", l)])
            for (kind, seg, gi, plus1) in ((0, 1, 0, True), (1, 2, 2, False), (2, 4, 1, True), (3, 5, 3, False)):
                src = modT[:, l, seg * 8:(seg + 1) * 8, :]
                gb = gT_sb[:, l, gi, :].unsqueeze(2).to_broadcast([128, 8, 4])
                if plus1:
                    do("dve", "tensor_scalar", out=tmp84, in0=src, scalar1=1.0, scalar2=None, op0=ALU.add,
                       R=[("modT", l)], W=["tmp84"])
                    do("dve", "tensor_tensor", out=cols[:, l, kind], in0=tmp84, in1=gb, op=ALU.mult,
                       R=["tmp84", "gT_sb"], W=[("cols", l)])
                else:
                    do("dve", "tensor_tensor", out=cols[:, l, kind], in0=src, in1=gb, op=ALU.mult,
                       R=[("modT", l), "gT_sb"], W=[("cols", l)])
            for v, (gsel, wsel) in enumerate(((0, 0), (1, 1), (0, 2), (1, 3))):
                do("dve", "tensor_tensor", out=tab_sb[:, :, v, :], in0=rope_gen[:, :, gsel, :],
                   in1=qk_b[:, l, wsel, :].unsqueeze(1).to_broadcast([128, 17, 64]), op=ALU.mult,
                   R=["rope_gen", "qk_b"], W=["tab_sb"])
            do("dve", "tensor_copy", out=tab_sb[:, :, 4:6, :], in_=rope_gen, R=["rope_gen"], W=["tab_sb"])
            do("sp", "dma_start", out=tabs_s[l].rearrange("t p f -> p t f"),
               in_=tab_sb.rearrange("p t v d -> p t (v d)"), R=["tab_sb"], W=[("tabs", l)], K=("tabs", l))

        mod_layer(0)
        for l in range(1, L):
            mod_layer(l)
        def late_convs():
            POOL_MAX_INFLIGHT[0] = 2
            conv(w_up_s[0], w_up_d[0], D, ("cv_up", 0))
            conv(w_down_s[0], w_down_d[0], DFF, ("cv_down", 0))
            for l in range(1, L):
                conv(w_in_s[l], w_in_d[l], D, ("cv_in", l))
                conv(w_out_s[l], w_out_d[l], D, ("cv_out", l))
                conv(w_up_s[l], w_up_d[l], D, ("cv_up", l))
                conv(w_down_s[l], w_down_d[l], DFF, ("cv_down", l))

        if DEV.get("dump"):
            dbg("modT", modT, [128, L, 48, 4], F32, [("modT", l) for l in range(L)])
            dbg("cols", cols, [128, L, 4, 8, 4], F32, [("cols", l) for l in range(L)])
            dbg("sT", sT, [128, 8, 4], F32, ["sT"])
        A.reset(persist_mark)
        S.barrier(skip=lambda k: isinstance(k, tuple) and str(k[0]).startswith('cv_'))

        tab_ctr = [0]

        def load_tab(l, tsel):
            slot = tab_ctr[0] % 3
            tab_ctr[0] += 1
            do("sp", "dma_start", out=tabslot[:, slot].rearrange("p v d -> p (v d)"), in_=tabs_s[l, tsel],
               R=[("tabs", l)], W=[("tabslot", slot)], K=("tabslot", slot))
            return slot

        def rms_rstd(src_ap, n, ss_ap, out_ap, junk, res_src, res_out, banks=()):
            do("act", "activation", out=junk, in_=src_ap, func=AF.Square, accum_out=ss_ap,
               R=res_src, W=["junk", (res_out, "ss")] + [BK(x) for x in banks])
            do("act", "activation", out=ss_ap, in_=ss_ap, func=AF.Ln, scale=1.0 / n, bias=EPS,
               R=[(res_out, "ss")], W=[(res_out, "ss")])
            do("act", "activation", out=out_ap, in_=ss_ap, func=AF.Exp, scale=-0.5, R=[(res_out, "ss")], W=[res_out])

        for b in range(BPC):
            do("sp", "dma_start", out=h_sb, in_=x_in[b].rearrange("(t p) d -> p t d", p=128),
               W=[("h", t) for t in range(NT)], K="hload")
            do("sp", "dma_start", out=hc_sb, in_=ctx_in[b].rearrange("(t p) d -> p t d", p=128),
               W=[("hc", t) for t in range(NCT)], K="hload")

            for l in range(L):
                last = is_last[l]
                m0 = A.mark()

                def hres(tt):
                    return ("hc", tt) if tt < NCT else ("h", tt - NCT)

                def hsrc(tt):
                    return hc_sb[:, tt, :] if tt < NCT else h_sb[:, tt - NCT, :]

                def rsel(tt, b=b):
                    return 2 if tt < NCT else b

                def colap(kind, c, r, l=l):
                    return cols[:, l, kind, c, r:r + 1]

                def shap(seg, c, r, l=l):
                    return modT[:, l, seg * 8 + c, r:r + 1]

                diagb4 = [A.alloc([128], F32) for _ in range(4)]
                for gi, (kind, r) in enumerate(((1, b), (3, b), (1, 2), (3, 2))):
                    if r == 2 and last:
                        continue
                    for half in range(2):
                        for c4 in range(4):
                            c = half * 4 + c4
                            do("dve", "tensor_scalar", out=diagb4[c4], in0=ident_f, scalar1=colap(kind, c, r), scalar2=None,
                               op0=ALU.mult, R=["ident_f", ("cols", l)], W=[("diagb", c4)])
                            do("pe", "matmul", bank(gi % 2)[:, c4 * 128:(c4 + 1) * 128], lhsT=ones_f, rhs=diagb4[c4],
                               start=True, stop=True, R=["ones_f", ("diagb", c4)], W=[BK(gi % 2)])
                        do("act", "activation", out=Grow[:, gi, half * 512:(half + 1) * 512], in_=bank(gi % 2), func=AF.Copy,
                           W=[BK(gi % 2), ("Grow", gi)])

                kT_sb = A.alloc([2, NTT * 128], BF16)
                V_sb = A.alloc([NTT, 4, 66], BF16)
                w_kv = A.alloc([8, 512], BF16)
                w_q = A.alloc([8, 1024], BF16)
                w_o = A.alloc([8, 1024], BF16)
                hn2 = [A.alloc([D], BF16) for _ in range(3)]
                junk = A.alloc([D], BF16)
                uT2 = [A.alloc([8, 128], BF16) for _ in range(3)]
                sq_f = A.alloc([512], F32)
                t1 = A.alloc([16, 64], F32)
                t2 = A.alloc([16, 64], F32)
                q_tok = A.alloc([8, 2, 64], BF16)
                k_tok2 = [A.alloc([2, 2, 64], BF16) for _ in range(3)]
                qT2 = [A.alloc([16, 128], BF16) for _ in range(2)]
                NPT = 4
                PT = [A.alloc([512], BF16) for _ in range(NPT)]
                O_tok = A.alloc([16, 64], BF16)
                OT_sb = A.alloc([8, 128], BF16)
                tmpf = t1.rearrange("p h d -> p (h d)")
                ssm = A.alloc([8], F32)
                ss1 = A.alloc([NTT], F32)
                ssk2 = A.alloc([3, 2], F32)
                rstd_k2 = A.alloc([3, 2], F32)
                ssq = A.alloc([8], F32)
                rstd_q = A.alloc([8], F32)
                den = A.alloc([4], F32)
                rden = A.alloc([4], F32)

                wsrc = w_in_s[l].rearrange("(c p) f -> p c f", p=128)
                for i, c0 in enumerate((512, 1280, 640, 1408)):
                    do("sp", "dma_start", out=w_kv[:, :, i * 128:(i + 1) * 128], in_=wsrc[:, :, c0:c0 + 128],
                       R=cv_res("cv_in", l, D), W=["w_kv"], K="w_kv")
                for i, c0 in enumerate((0, 768)):
                    do("sp", "dma_start", out=w_q[:, :, i * 512:(i + 1) * 512], in_=wsrc[:, :, c0:c0 + 512],
                       R=cv_res("cv_in", l, D), W=["w_q"], K="w_q")
                do("sp", "dma_start", out=w_o, in_=w_out_s[l].rearrange("(c p) f -> p c f", p=128),
                   R=cv_res("cv_out", l, D), W=["w_o"], K="w_o")
                if b == 0 and l == 0:
                    S.op("pool", None, reads=["w_kv", "w_q", "w_o"] + [("h", t) for t in range(NT)] + [("hc", t) for t in range(NCT)])
                    late_convs()
                do("dve", "memset", V_sb, 1.0, W=["V_sb"])
                do("dve", "memset", qT2[0], 0.0, W=[("qT", 0)])
                do("dve", "memset", qT2[1], 0.0, W=[("qT", 1)])

                def front_a(tt, rstd_ap, rstd_res, have_rstd, tb, hn_ap, hn_res, junk_ap, ss_ap, hn_on_act=False):
                    if not have_rstd:
                        rms_rstd(hsrc(tt), D, ss_ap, rstd_ap, junk_ap, [hres(tt)], rstd_res)
                    if hn_on_act:
                        do("act", "activation", out=hn_ap, in_=hsrc(tt), func=AF.Copy, scale=rstd_ap,
                           R=[hres(tt), rstd_res], W=[hn_res])
                    else:
                        do("dve", "tensor_scalar", out=hn_ap, in0=hsrc(tt), scalar1=rstd_ap, scalar2=None, op0=ALU.mult,
                           R=[hres(tt), rstd_res], W=[hn_res])
                    tbv = bank_bf(tb)
                    for c in range(8):
                        do("pe", "transpose", out=tbv[:, c, :], in_=hn_ap[:, c * 128:(c + 1) * 128], identity=ident_bf,
                           R=[hn_res, "ident_bf"], W=[BK(tb)])

                def front_b(tt, acol_kind, sh_seg, tb, dst, dst_res):
                    r = rsel(tt)
                    tbv = bank_bf(tb)
                    for c in range(8):
                        do("dve", "tensor_scalar", out=dst[:, c, :], in0=tbv[:, c, :], scalar1=colap(acol_kind, c, r),
                           scalar2=shap(sh_seg, c, r), op0=ALU.mult, op1=ALU.add,
                           R=[("cols", l), ("modT", l)], W=[BK(tb), dst_res])

                def rope(src_ps, nh, slot, vC, vS, out_ap, rstd_bc, rstd_res, banks, out_res):
                    C = tabslot[:, slot, vC, :]
                    Sg = tabslot[:, slot, vS, :].rearrange("p (a b d) -> p a b d", a=2, b=2)
                    bw = [BK(x) for x in banks]
                    t1v = t1[:, 0:nh, :]
                    t2f = t2[:, 0:nh, :]
                    t2v = t2f.rearrange("p h (a b d) -> p h a b d", a=2, b=2)
                    s5 = src_ps.rearrange("p h (a b d) -> p h a b d", a=2, b=2)
                    do("dve", "tensor_tensor", out=t1v, in0=src_ps, in1=C.unsqueeze(1).to_broadcast([128, nh, 64]), op=ALU.mult,
                       R=[("tabslot", slot)], W=bw + ["t1"])
                    do("dve", "tensor_tensor", out=t2v[:, :, :, 0, :], in0=s5[:, :, :, 1, :],
                       in1=Sg[:, :, 0, :].unsqueeze(1).to_broadcast([128, nh, 2, 16]), op=ALU.mult,
                       R=[("tabslot", slot)], W=bw + ["t2"])
                    do("dve", "tensor_tensor", out=t2v[:, :, :, 1, :], in0=s5[:, :, :, 0, :],
                       in1=Sg[:, :, 1, :].unsqueeze(1).to_broadcast([128, nh, 2, 16]), op=ALU.mult,
                       R=[("tabslot", slot)], W=bw + ["t2"])
                    if rstd_bc is None:
                        do("dve", "tensor_tensor", out=out_ap, in0=t1v, in1=t2f, op=ALU.add, R=["t1", "t2"], W=[out_res])
                    else:
                        do("dve", "tensor_tensor", out=t1v, in0=t1v, in1=t2f, op=ALU.add, R=["t1", "t2"], W=["t1"])
                        do("dve", "tensor_tensor", out=out_ap, in0=t1v, in1=rstd_bc, op=ALU.mult, R=["t1", rstd_res], W=[out_res])

                def head_rstd_a(src_ps, nh, ss_ap, ss_res, banks):
                    bw = [BK(x) for x in banks]
                    do("act", "activation", out=sq_f[:, 0:nh * 64], in_=src_ps, func=AF.Square, W=bw + ["sq_f"])
                    do("dve", "tensor_reduce", out=ss_ap, in_=sq_f[:, 0:nh * 64].rearrange("p (h d) -> p h d", d=64),
                       axis=AX.X, op=ALU.add, R=["sq_f"], W=[ss_res])

                def head_rstd_b(ss_ap, ss_res, out_ap, out_res):
                    do("act", "activation", out=ss_ap, in_=ss_ap, func=AF.Ln, scale=1.0 / 64, bias=EPS, R=[ss_res], W=[ss_res])
                    do("act", "activation", out=out_ap, in_=ss_ap, func=AF.Exp, scale=-0.5, R=[ss_res], W=[out_res])

                slots1 = {}
                TB3 = [0, 1, 6]
                KV3 = [2, 3, 7]
                sqk = [A.alloc([128], F32) for _ in range(3)]

                def p1R(tt):
                    rms_rstd(hsrc(tt), D, ss1[:, tt:tt + 1], rstd1[:, tt:tt + 1], junk, [hres(tt)], ("rstd1", tt))

                def p1A1(tt):
                    pb = tt % 3
                    do("act", "activation", out=hn2[pb], in_=hsrc(tt), func=AF.Copy, scale=rstd1[:, tt:tt + 1],
                       R=[hres(tt), ("rstd1", tt)], W=[("hn", pb)])

                def p1A2(tt):
                    pb = tt % 3
                    tbv = bank_bf(TB3[pb])
                    for c in range(8):
                        do("pe", "transpose", out=tbv[:, c, :], in_=hn2[pb][:, c * 128:(c + 1) * 128], identity=ident_bf,
                           R=[("hn", pb), "ident_bf"], W=[BK(TB3[pb])])

                def p1A3(tt):
                    pb = tt % 3
                    front_b(tt, 0, 0, TB3[pb], uT2[pb], ("uT", pb))

                def p1B1a(tt):
                    pb = tt % 3
                    kvb = KV3[pb]
                    slots1[tt] = load_tab(l, 16 if tt < NCT else tt - NCT)
                    uT = uT2[pb]
                    for c in range(8):
                        do("pe", "matmul", bank(kvb), lhsT=uT[:, c, :], rhs=w_kv[:, c, :], start=(c == 0), stop=(c == 7),
                           R=[("uT", pb), "w_kv"], W=[BK(kvb)])

                def p1B1b(tt):
                    pb = tt % 3
                    kvb = KV3[pb]
                    kv = bank(kvb)
                    do("act", "activation", out=sqk[pb], in_=kv[:, 0:128], func=AF.Square, W=[BK(kvb), ("sqk", pb)])
                    do("act", "activation", out=V_sb[:, tt, :, 0:64], in_=kv[:, 256:512].rearrange("p (h d) -> p h d", d=64),
                       func=AF.Copy, W=[BK(kvb), "V_sb"])
                    do("dve", "tensor_reduce", out=ssk2[:, pb], in_=sqk[pb].rearrange("p (h d) -> p h d", d=64),
                       axis=AX.X, op=ALU.add, R=[("sqk", pb)], W=[("ssk", pb)])

                def p1B2(tt):
                    pb = tt % 3
                    kvb = KV3[pb]
                    slot = slots1[tt]
                    k_tok = k_tok2[pb]
                    kv = bank(kvb)
                    head_rstd_b(ssk2[:, pb], ("ssk", pb), rstd_k2[:, pb], ("rstd_k", pb))
                    rope(kv[:, 128:256].rearrange("p (h d) -> p h d", d=64), 2, slot, 4, 5, k_tok[:, :, 1, :], None, None, [kvb], ("k_tok", pb))
                    rope(kv[:, 0:128].rearrange("p (h d) -> p h d", d=64), 2, slot, 2, 3, k_tok[:, :, 0, :],
                         rstd_k2[:, pb].unsqueeze(2).to_broadcast([128, 2, 64]), ("rstd_k", pb), [kvb], ("k_tok", pb))

                def p1B3a(tt):
                    pb = tt % 3
                    ktbank = 4 + tt % 2
                    ktb = bank_bf(ktbank)
                    for j in range(2):
                        do("pe", "transpose", out=ktb[:, j, :], in_=k_tok2[pb][:, j].rearrange("p a d -> p (a d)"), identity=ident_bf,
                           R=[("k_tok", pb), "ident_bf"], W=[BK(ktbank)])

                def p1B3b(tt):
                    ktbank = 4 + tt % 2
                    ktb = bank_bf(ktbank)
                    do("act", "activation", out=kT_sb[:, :, tt * 128:(tt + 1) * 128], in_=ktb[:, 0:2, :], func=AF.Copy,
                       W=[BK(ktbank), "kT_sb"])

                stages = [(7, p1R), (5, p1A1), (4, p1A2), (3, p1A3), (2, p1B1a), (1, p1B1b), (0, p1B2), (-1, p1B3a), (-2, p1B3b)]
                for it in range(-7, NTT + 2):
                    for off, fn_ in stages:
                        if 0 <= it + off < NTT:
                            fn_(it + off)

                if DEV.get("dump") and b == 0 and l == 0:
                    dbg("rstd1", rstd1, [128, NTT], F32, [("rstd1", t) for t in range(NTT)])
                    dbg("kT", kT_sb, [128, 2, NTT * 128], BF16, ["kT_sb"])
                    dbg("V", V_sb, [128, NTT, 4, 66], BF16, ["V_sb"])

                q_tiles = list(range(NCT, NTT)) + ([] if last else list(range(NCT)))
                if DEV.get("max_q_tiles") is not None:
                    q_tiles = q_tiles[:DEV["max_q_tiles"]]
                SBANKS = [3, 4, 5]
                TBANK = 0
                tabs_of = {}
                uTq = uT2[0]

                def prepA0(tt):
                    tabs_of[tt] = load_tab(l, 16 if tt < NCT else tt - NCT)
                    do("dve", "tensor_scalar", out=hn2[0], in0=hsrc(tt), scalar1=rstd1[:, tt:tt + 1], scalar2=None, op0=ALU.mult,
                       R=[hres(tt), ("rstd1", tt)], W=[("hn", 0)])

                def prepA(tt):
                    tbv_ = bank_bf(TBANK)
                    for c in range(8):
                        do("pe", "transpose", out=tbv_[:, c, :], in_=hn2[0][:, c * 128:(c + 1) * 128], identity=ident_bf,
                           R=[("hn", 0), "ident_bf"], W=[BK(TBANK)])
                    front_b(tt, 0, 0, TBANK, uTq, ("uT", 0))

                def prepB(tt):
                    for n in range(2):
                        for c in range(8):
                            do("pe", "matmul", bank(1 + n), lhsT=uTq[:, c, :], rhs=w_q[:, c, n * 512:(n + 1) * 512],
                               start=(c == 0), stop=(c == 7), R=[("uT", 0), "w_q"], W=[BK(1 + n)])

                def prepC(tt):
                    head_rstd_a(bank(1), 8, ssq, "ssq", [1])

                def prepD(tt):
                    slot = tabs_of[tt]
                    head_rstd_b(ssq, "ssq", rstd_q, "rstd_q")
                    rope(bank(1).rearrange("p (h d) -> p h d", d=64), 8, slot, 0, 1, q_tok[:, :, 0, :],
                         rstd_q.unsqueeze(2).to_broadcast([128, 8, 64]), "rstd_q", [1], "q_tok")
                    rope(bank(2).rearrange("p (h d) -> p h d", d=64), 8, slot, 4, 5, q_tok[:, :, 1, :], None, None, [2], "q_tok")

                def prepE(tt):
                    qtb = bank_bf(TBANK)
                    for j in range(8):
                        do("pe", "transpose", out=qtb[:, j, :], in_=q_tok[:, j].rearrange("p a d -> p (a d)"), identity=ident_bf,
                           R=["q_tok", "ident_bf"], W=[BK(TBANK)])

                def prepF(tt, qb):
                    qtb = bank_bf(TBANK)
                    do("dve", "tensor_copy", out=qT2[qb][0:64, 0:8, :], in_=qtb[0:64, :, :], W=[BK(TBANK), ("qT", qb)])
                    do("dve", "tensor_copy", out=qT2[qb][64:128, 8:16, :], in_=qtb[64:128, :, :], W=[BK(TBANK), ("qT", qb)])

                def tailA(tt):
                    otb = bank_bf(TBANK)
                    Of = O_tok.rearrange("p h d -> p (h d)")
                    for c in range(8):
                        do("pe", "transpose", out=otb[:, c, :], in_=Of[:, c * 128:(c + 1) * 128], identity=ident_bf,
                           R=["O_tok", "ident_bf"], W=[BK(TBANK)])
                    do("dve", "tensor_copy", out=OT_sb, in_=otb, W=[BK(TBANK), "OT_sb"])

                def tailB(tt):
                    for n in range(2):
                        for c in range(8):
                            do("pe", "matmul", bank(1 + n), lhsT=OT_sb[:, c, :], rhs=w_o[:, c, n * 512:(n + 1) * 512],
                               start=(c == 0), stop=(c == 7), R=["OT_sb", "w_o"], W=[BK(1 + n)])

                def tailC(tt):
                    mix = ps[:, 1:3, :].rearrange("p a b -> p (a b)")
                    rms_rstd(mix, D, ssm[:, 0:1], ssm[:, 1:2], junk, [], "rstd_m", banks=(1, 2))
                    gsel = 2 if tt < NCT else 0
                    do("dve", "scalar_tensor_tensor", out=tmpf, in0=mix, scalar=ssm[:, 1:2], in1=Grow[:, gsel, :],
                       op0=ALU.mult, op1=ALU.mult, R=["rstd_m", ("Grow", gsel)], W=[BK(1), BK(2), "tmpf"])
                    do("dve", "tensor_tensor", out=hsrc(tt), in0=hsrc(tt), in1=tmpf, op=ALU.add,
                       R=["tmpf", hres(tt)], W=[hres(tt)])

                def make_steps(tt):
                    is_ctx = tt < NCT
                    i_lat = tt - NCT
                    steps = []
                    for g in range(4):
                        ab, kvh = g // 2, g % 2
                        if is_ctx:
                            kbs = [(0, None), (1, None)]
                        elif ab == 0:
                            kbs = [(kb, None) for kb in range(NTT)]
                        else:
                            kbs = []
                            if i_lat > 0:
                                kbs.append((NCT + i_lat - 1, 0))
                            kbs.append((NCT + i_lat, None))
                            if i_lat < NT - 1:
                                kbs.append((NCT + i_lat + 1, 1))
                            kbs += [(0, None), (1, None)]
                        for ki, (kb, m) in enumerate(kbs):
                            steps.append(dict(g=g, ab=ab, kvh=kvh, kb=kb, m=m, first=(ki == 0), lastk=(ki == len(kbs) - 1)))
                    return steps

                sctr = [0]

                def attn(tt, qb, hooks):
                    steps = make_steps(tt)
                    ns = len(steps)
                    base = sctr[0]
                    sctr[0] += ns
                    qT_sb = qT2[qb]

                    def emit_S(si):
                        s = steps[si]
                        sb_ = SBANKS[(base + si) % 3]
                        h0 = s['ab'] * 8 + s['kvh'] * 4
                        rhs = qT_sb[:, h0:h0 + 4, :].rearrange("p h q -> p (h q)")
                        do("pe", "matmul", bank(sb_), lhsT=kT_sb[:, s['kvh'], s['kb'] * 128:(s['kb'] + 1) * 128], rhs=rhs,
                           start=True, stop=(s['m'] is None), R=["kT_sb", ("qT", qb)], W=[BK(sb_)])
                        if s['m'] is not None:
                            do("pe", "matmul", bank(sb_), lhsT=ident_bf, rhs=mask_bf[:, s['m'], :], start=False, stop=True,
                               R=["ident_bf", "mask_bf"], W=[BK(sb_)])

                    def emit_EXP_PV(si):
                        s = steps[si]
                        sb_ = SBANKS[(base + si) % 3]
                        pslot = (base + si) % NPT
                        ob = 6 + s['g'] % 2
                        obv = bank(ob).rearrange("p (h x) -> p h x", h=4)
                        do("act", "activation", out=PT[pslot], in_=bank(sb_), func=AF.Exp, scale=0.125,
                           W=[BK(sb_), ("PT", pslot)])
                        for hh in range(4):
                            do("pe", "matmul", obv[:, hh, 0:65], lhsT=PT[pslot][:, hh * 128:(hh + 1) * 128],
                               rhs=V_sb[:, s['kb'], s['ab'] * 2 + s['kvh'], 0:65],
                               start=(s['first'] and hh == 0), stop=s['lastk'], skip_group_check=True,
                               R=[("PT", pslot), "V_sb"], W=[BK(ob)])
                        if s['lastk']:
                            g = s['g']
                            if s['ab'] == 1:
                                do("dve", "tensor_tensor", out=den, in0=obv[:, :, 64], in1=sinkexp[:, l, s['kvh'] * 4:(s['kvh'] + 1) * 4],
                                   op=ALU.add, R=["sinkexp"], W=[BK(ob), "den"])
                                do("dve", "reciprocal", out=rden, in_=den, R=["den"], W=["rden"])
                            else:
                                do("dve", "reciprocal", out=rden, in_=obv[:, :, 64], W=[BK(ob), "rden"])
                            do("dve", "tensor_tensor", out=O_tok[:, g * 4:(g + 1) * 4, :], in0=obv[:, :, 0:64],
                               in1=rden.unsqueeze(2).to_broadcast([128, 4, 64]), op=ALU.mult,
                               R=["rden"], W=[BK(ob), "O_tok"])

                    for si in range(min(3, ns)):
                        emit_S(si)
                    for si in range(ns):
                        emit_EXP_PV(si)
                        if si + 3 < ns:
                            emit_S(si + 3)
                        for f in hooks.get(si, ()):
                            f()

                HK = DEV.get("hooks", dict(tA=3, tB=6, tC=9, pA=11, pB=16, pC=21, pD=23, pE=35, pF=38))
                if q_tiles:
                    t0_ = q_tiles[0]
                    prepA0(t0_); prepA(t0_); prepB(t0_); prepC(t0_); prepD(t0_); prepE(t0_); prepF(t0_, 0)
                for idx, tt in enumerate(q_tiles):
                    ns_ = len(make_steps(tt))
                    hooks = {}

                    def at(step, f, hooks=hooks, ns_=ns_):
                        hooks.setdefault(min(step, ns_ - 1), []).append(f)

                    if idx > 0:
                        prev = q_tiles[idx - 1]
                        at(0 if tt < NCT else HK['tA'], lambda prev=prev: tailA(prev))
                        at(HK['tB'], lambda prev=prev: tailB(prev))
                        at(HK['tC'], lambda prev=prev: tailC(prev))
                    if idx + 1 < len(q_tiles):
                        nxt = q_tiles[idx + 1]
                        nqb = (idx + 1) % 2
                        at(max(HK['pA'] - 3, 0), lambda nxt=nxt: prepA0(nxt))
                        at(HK['pA'], lambda nxt=nxt: prepA(nxt))
                        at(HK['pB'], lambda nxt=nxt: prepB(nxt))
                        at(HK['pC'], lambda nxt=nxt: prepC(nxt))
                        at(HK['pD'], lambda nxt=nxt: prepD(nxt))
                        at(HK['pE'], lambda nxt=nxt: prepE(nxt))
                        at(HK['pF'], lambda nxt=nxt, nqb=nqb: prepF(nxt, nqb))
                    attn(tt, idx % 2, hooks)
                if q_tiles:
                    tailA(q_tiles[-1]); tailB(q_tiles[-1]); tailC(q_tiles[-1])

                if DEV.get("dump") and b == 0 and l == 0:
                    dbg("h_p2", h_sb, [128, NT, D], F32, [("h", t) for t in range(NT)])
                    dbg("hc_p2", hc_sb, [128, NCT, D], F32, [("hc", t) for t in range(NCT)])

                A.reset(m0)
                S.barrier(skip=lambda k: isinstance(k, tuple) and str(k[0]).startswith('cv_'))
                if not DEV.get("skip_mlp"):
                    mlp_tiles = list(range(NCT, NTT)) + ([] if last else list(range(NCT)))
                    ntl = len(mlp_tiles)
                    u2T = A.alloc([8, NTT * 128], BF16)
                    wu_buf = [A.alloc([8, 512], BF16) for _ in range(3)]
                    wd_buf = [A.alloc([4, D], BF16) for _ in range(3)]
                    hr_f = [A.alloc([384], F32) for _ in range(2)]
                    hT = [A.alloc([384], BF16) for _ in range(2)]
                    hn3 = A.alloc([D], BF16)
                    junk3 = A.alloc([D], BF16)
                    tmp3 = A.alloc([D], F32)
                    ssy = A.alloc([8], F32)
                    hn3 = [hn3, A.alloc([D], BF16)]
                    ss3 = A.alloc([NTT], F32)
                    for ti, tt in enumerate(mlp_tiles):
                        front_a(tt, rstd2[:, tt:tt + 1], ("rstd2", tt), False, ti % 2, hn3[ti % 2], ("hn", ti % 2), junk3, ss3[:, tt:tt + 1], hn_on_act=True)
                        front_b(tt, 2, 3, ti % 2, u2T[:, :, ti * 128:(ti + 1) * 128], ("u2T", ti))
                    if ntl == 18:
                        blocks = [list(range(i, i + 3)) for i in range(0, 18, 3)]
                    else:
                        blocks = [[0, 1, 2], [3, 4, 5], [6, 7, 8], [9, 10, 11], [12, 13], [14, 15]]
                    if DEV.get("max_blocks") is not None:
                        blocks = blocks[:DEV["max_blocks"]]
                    wctr = [0]
                    for blk in blocks:
                        nt_b = len(blk)
                        ntok = nt_b * 128
                        tok0 = blk[0] * 128
                        wslots = {}

                        def emit_U(f):
                            fg, fl = f // 4, f % 4
                            if fl == 0:
                                wslot = wctr[0] % 3
                                wctr[0] += 1
                                do("sp", "dma_start", out=wu_buf[wslot],
                                   in_=w_up_s[l][:, fg * 512:(fg + 1) * 512].rearrange("(c p) f -> p c f", p=128),
                                   R=cv_res("cv_up", l, D), W=[("wu", wslot)], K=("wu", wslot))
                                do("sp", "dma_start", out=wd_buf[wslot],
                                   in_=w_down_s[l][fg * 512:(fg + 1) * 512, :].rearrange("(fl p) n -> p fl n", p=128),
                                   R=cv_res("cv_down", l, DFF), W=[("wd", wslot)], K=("wd", wslot))
                                wslots[fg] = wslot
                            ws = wslots[fg]
                            ub = f % 2
                            for c in range(8):
                                do("pe", "matmul", bank(ub)[:, 0:ntok], lhsT=wu_buf[ws][:, c, fl * 128:(fl + 1) * 128],
                                   rhs=u2T[:, c, tok0:tok0 + ntok], start=(c == 0), stop=(c == 7),
                                   R=[("wu", ws)] + [("u2T", ti) for ti in blk], W=[BK(ub)])
                            do("act", "activation", out=hr_f[ub][:, 0:ntok], in_=bank(ub)[:, 0:ntok], func=AF.Relu,
                               W=[BK(ub), ("hr", ub)])
                            do("act", "activation", out=hT[ub][:, 0:ntok], in_=hr_f[ub][:, 0:ntok], func=AF.Square,
                               R=[("hr", ub)], W=[("hT", ub)])

                        def emit_D(f):
                            fg, fl = f // 4, f % 4
                            ws = wslots[fg]
                            ub = f % 2
                            for t in range(nt_b):
                                for n in range(2):
                                    yb = 2 + t * 2 + n
                                    do("pe", "matmul", bank(yb), lhsT=hT[ub][:, t * 128:(t + 1) * 128],
                                       rhs=wd_buf[ws][:, fl, n * 512:(n + 1) * 512], start=(f == 0), stop=(f == 31),
                                       R=[("hT", ub), ("wd", ws)], W=[BK(yb)])

                        emit_U(0)
                        for f in range(32):
                            if f + 1 < 32:
                                emit_U(f + 1)
                            emit_D(f)
                        for t, ti in enumerate(blk):
                            tt = mlp_tiles[ti]
                            y = ps[:, 2 + 2 * t:4 + 2 * t, :].rearrange("p a b -> p (a b)")
                            rms_rstd(y, D, ssy[:, 0:1], ssy[:, 1:2], junk3, [], "rstd_y", banks=(2 + 2 * t, 3 + 2 * t))
                            gsel = 3 if tt < NCT else 1
                            do("dve", "scalar_tensor_tensor", out=tmp3, in0=y, scalar=ssy[:, 1:2], in1=Grow[:, gsel, :],
                               op0=ALU.mult, op1=ALU.mult, R=["rstd_y", ("Grow", gsel)],
                               W=[BK(2 + 2 * t), BK(3 + 2 * t), "tmpf"])
                            do("dve", "tensor_tensor", out=hsrc(tt), in0=hsrc(tt), in1=tmp3, op=ALU.add,
                               R=["tmpf", hres(tt)], W=[hres(tt)])
                A.reset(m0)
                S.barrier(skip=lambda k: isinstance(k, tuple) and str(k[0]).startswith('cv_'))

            do("sp", "dma_start", out=h_out[b].rearrange("(t p) d -> p t d", p=128), in_=h_sb,
               R=[("h", t) for t in range(NT)], K="out")
            if emit_hc:
                do("sp", "dma_start", out=hc_out[b].rearrange("(t p) d -> p t d", p=128), in_=hc_sb,
                   R=[("hc", t) for t in range(NCT)], K="out")

        S.wait_dma_all("sp", "out")
        if dbg_outs:
            S.wait_dma_all("sp", "dbg")
        S.emit()
        build_program.stats = S.stats
    return nc, dbg_outs


def _rope_tables():
    theta = np.float32(10000.0)
    axis_dim = 32
    inv = (theta ** (-(np.arange(0, axis_dim, 2, dtype=np.float32) / np.float32(axis_dim)))).astype(np.float32)
    tok = np.arange(SEQ)
    row = (tok // 64).astype(np.float32)
    col = (tok % 64).astype(np.float32)
    ang_r = row[:, None] * inv[None, :]
    ang_c = col[:, None] * inv[None, :]
    cr, sr, cc, sc = np.cos(ang_r), np.sin(ang_r), np.cos(ang_c), np.sin(ang_c)
    C = np.concatenate([cr, cr, cc, cc], axis=1).astype(np.float32)
    Sg = np.concatenate([-sr, sr, -sc, sc], axis=1).astype(np.float32)
    out = np.zeros((128, 17, 2, 64), np.float32)
    out[:, :16, 0, :] = C.reshape(16, 128, 64).transpose(1, 0, 2)
    out[:, :16, 1, :] = Sg.reshape(16, 128, 64).transpose(1, 0, 2)
    out[:, 16, 0, :] = 1.0
    return out


def _masks():
    kk = np.arange(128)[:, None]
    r = np.arange(128)[None, :]
    mprev = np.where(kk >= r, 0.0, NEG).astype(np.float32)
    mnext = np.where(kk <= r, 0.0, NEG).astype(np.float32)
    m = np.zeros((128, 2, 512), np.float32)
    m[:, 0, :] = np.tile(mprev, (1, 4))
    m[:, 1, :] = np.tile(mnext, (1, 4))
    return m


def _swap16(v):
    v4 = v.reshape(v.shape[:-1] + (2, 2, 16))
    return np.ascontiguousarray(v4[..., ::-1, :]).reshape(v.shape)


def _common_inputs(ls, c, c_ctx, w_ada, b_ada, g_pre_mix, g_post_mix, g_pre_mlp, g_post_mlp,
                   w_in, q_norm, k_norm, sink, w_out, w_up, w_down):
    L = len(ls)
    f = lambda a: np.ascontiguousarray(np.asarray(a, dtype=np.float32))
    sl = slice(ls[0], ls[-1] + 1)
    b_adaT = f(np.asarray(b_ada)[sl].reshape(L, 48, 128).transpose(2, 0, 1))
    g4 = np.stack([np.asarray(g_pre_mix)[sl], np.asarray(g_pre_mlp)[sl], np.asarray(g_post_mix)[sl], np.asarray(g_post_mlp)[sl]], axis=1)
    gT = f(g4.reshape(L, 4, 8, 128).transpose(3, 0, 1, 2))
    qn = np.asarray(q_norm)[sl]
    kn = np.asarray(k_norm)[sl]
    qk = f(np.stack([qn, _swap16(qn), kn, _swap16(kn)], axis=1).reshape(-1))
    common = {
        "w_ada": f(np.asarray(w_ada)[sl]), "b_adaT": b_adaT, "gT": gT, "qk_norm": qk,
        "sink": f(np.asarray(sink)[sl].reshape(-1)),
        "w_in": f(np.asarray(w_in)[sl]), "w_out": f(np.asarray(w_out)[sl]),
        "w_up": f(np.asarray(w_up)[sl]), "w_down": f(np.asarray(w_down)[sl]),
        "ident": np.eye(128, dtype=np.float32), "rope_cs": _rope_tables(), "masks": _masks(),
    }
    cTs = []
    c = np.asarray(c, dtype=np.float32)
    c_ctx = np.asarray(c_ctx, dtype=np.float32)
    for core in range(NCORES):
        rows = np.zeros((4, D), np.float32)
        rows[0] = c[core * BPC]
        rows[1] = c[core * BPC + 1]
        rows[2] = c_ctx
        cTs.append(f(rows.reshape(4, 8, 128).transpose(2, 1, 0)))
    return common, cTs


_PROG_CACHE = {}


def _get_prog(L, is_last, emit_hc):
    key = (L, tuple(is_last), emit_hc, repr(sorted(DEV.items())))
    if key not in _PROG_CACHE:
        _PROG_CACHE[key] = build_program(L, list(is_last), emit_hc)
    return _PROG_CACHE[key]


def _launch(ls, is_last, emit_hc, h, hc, kw):
    nc, dbg_names = _get_prog(len(ls), is_last, emit_hc)
    common, cTs = _common_inputs(ls, **kw)
    in_maps = []
    for core in range(NCORES):
        m = dict(common)
        m["x_in"] = np.ascontiguousarray(h[core * BPC:(core + 1) * BPC])
        m["ctx_in"] = np.ascontiguousarray(hc[core * BPC:(core + 1) * BPC])
        m["cT"] = cTs[core]
        in_maps.append(m)
    res = run_bass_kernel_spmd(nc, in_maps, core_ids=list(range(NCORES)))
    h_new = np.concatenate([r["h_out"] for r in res.results], axis=0)
    hc_new = None
    if emit_hc:
        hc_new = np.concatenate([r["hc_out"] for r in res.results], axis=0)
    return h_new, hc_new, res


FUSED = True


def kernel(x, c, ctx, c_ctx, w_ada, b_ada, g_pre_mix, g_post_mix, g_pre_mlp, g_post_mlp,
           w_in, q_norm, k_norm, sink, w_out, w_up, w_down):
    kw = dict(c=c, c_ctx=c_ctx, w_ada=w_ada, b_ada=b_ada, g_pre_mix=g_pre_mix, g_post_mix=g_post_mix,
              g_pre_mlp=g_pre_mlp, g_post_mlp=g_post_mlp, w_in=w_in, q_norm=q_norm, k_norm=k_norm,
              sink=sink, w_out=w_out, w_up=w_up, w_down=w_down)
    h = np.asarray(x, dtype=np.float32)
    hc = np.asarray(ctx, dtype=np.float32)
    if FUSED:
        h, _, _ = _launch([0, 1], [False, True], False, h, hc, kw)
    else:
        h, hc, _ = _launch([0], [False], True, h, hc, kw)
        h, _, _ = _launch([1], [True], False, h, hc, kw)
    return h.astype(np.float32)
```
